# Optimizing a Trainium2 kernel written in Bass

```python
import jax, jax.numpy as jnp
from jax import lax
import numpy as np

D_MODEL = 2048
BATCH = 4
SEQ = 2048
DEPTH = 2
DEC_BATCH = 128
DEC_SEQ = 8
PAST_LEN = 16384
PAGE_SIZE = 128

D_POOL = D_MODEL // 4
D_SCONV = 3 * D_MODEL // 8
D_CCONV = D_MODEL - D_POOL - D_SCONV
POOL_WINDOWS = (2, 4, 8, 16)
N_POOL_GROUPS = len(POOL_WINDOWS)
POOL_GROUP = D_POOL // N_POOL_GROUPS
POOL_BUF = max(POOL_WINDOWS) - 1
SCONV_K = 3
CCONV_K = 31
FFN_CONV_K = 3
D_FF = 5632
D_IN = D_POOL + 3 * D_SCONV + 2 * D_CCONV
EPS = 1e-6

kernel_name = "hybrid_pool_shortconv_conformer_decoder_step"


def rmsnorm(x, g):
    xf = x.astype(jnp.float32)
    r = lax.rsqrt(jnp.mean(xf * xf, axis=-1, keepdims=True) + EPS)
    return (xf * r * g.astype(jnp.float32)).astype(x.dtype)


def causal_dwconv(buf, u, w):
    k = w.shape[0]
    c = u.shape[-1]
    ext = jnp.concatenate([buf.astype(u.dtype), u], axis=1)
    y = lax.conv_general_dilated(
        ext, w[:, None, :].astype(u.dtype), window_strides=(1,), padding='VALID',
        dimension_numbers=('NWC', 'WIO', 'NWC'), feature_group_count=c)
    return y, ext[:, ext.shape[1] - (k - 1):]


def multiscale_pool(buf, u, pos0, w_grp, scale):
    b, t, _ = u.shape
    ext = jnp.concatenate([buf.astype(u.dtype), u], axis=1)
    cs = jnp.cumsum(ext.astype(jnp.float32), axis=1)
    cs = jnp.pad(cs, ((0, 0), (1, 0), (0, 0)))
    pos = pos0 + jnp.arange(t, dtype=jnp.int32)
    outs = []
    for g, k in enumerate(POOL_WINDOWS):
        lo, hi = g * POOL_GROUP, (g + 1) * POOL_GROUP
        s = cs[:, POOL_BUF + 1:POOL_BUF + 1 + t, lo:hi] - cs[:, POOL_BUF + 1 - k:POOL_BUF + 1 - k + t, lo:hi]
        cnt = jnp.minimum(pos + 1, k).astype(jnp.float32)[None, :, None]
        outs.append(s / cnt)
    pooled = jnp.stack(outs, axis=2)
    d = pooled - u.astype(jnp.float32).reshape(b, t, N_POOL_GROUPS, POOL_GROUP)
    y = jnp.einsum('btgc,gcd->btgd', d, w_grp.astype(jnp.float32)).reshape(b, t, D_POOL)
    y = y * scale.astype(jnp.float32)
    return y.astype(u.dtype), ext[:, ext.shape[1] - POOL_BUF:]


def hybrid_layer(x, st_pool, st_sconv, st_cconv, st_ffn, pos0,
                 norm1_g, w_in, pool_w, pool_scale, sconv_w, cconv_w, cconv_b, cconv_norm_g,
                 w_out, norm2_g, w_up, ffn_conv_w, w_down):
    h = rmsnorm(x, norm1_g)
    proj = jnp.einsum('btd,de->bte', h, w_in.astype(h.dtype))
    o1 = D_POOL
    o2 = o1 + D_SCONV
    o3 = o2 + D_SCONV
    o4 = o3 + D_SCONV
    o5 = o4 + D_CCONV
    u_pool = proj[..., :o1]
    gate_b, gate_c, h_b = proj[..., o1:o2], proj[..., o2:o3], proj[..., o3:o4]
    glu_a, glu_g = proj[..., o4:o5], proj[..., o5:]

    y_pool, new_pool = multiscale_pool(st_pool, u_pool, pos0, pool_w, pool_scale)
    c_b, new_sconv = causal_dwconv(st_sconv, gate_c * h_b, sconv_w)
    y_sconv = gate_b * c_b
    v = glu_a * jax.nn.sigmoid(glu_g)
    c_c, new_cconv = causal_dwconv(st_cconv, v, cconv_w)
    y_cconv = jax.nn.silu(rmsnorm(c_c + cconv_b.astype(c_c.dtype), cconv_norm_g))

    mix = jnp.concatenate([y_pool, y_sconv, y_cconv], axis=-1)
    x = x + jnp.einsum('btd,de->bte', mix, w_out.astype(mix.dtype))

    h2 = rmsnorm(x, norm2_g)
    up = jnp.einsum('btd,df->btf', h2, w_up.astype(h2.dtype))
    up_c, new_ffn = causal_dwconv(st_ffn, up, ffn_conv_w)
    act = jax.nn.silu(up_c[..., :D_FF]) * up_c[..., D_FF:]
    x = x + jnp.einsum('btf,fd->btd', act, w_down.astype(act.dtype))
    return x, new_pool, new_sconv, new_cconv, new_ffn


def setup_inputs(seed: int = 0) -> dict:
    key = jax.random.key(seed)
    ks = jax.random.split(key, 24)
    f32 = jnp.float32
    nrm = lambda k, shape, s: jax.random.normal(k, shape, f32) * s
    return {
        "x_prompt": nrm(ks[0], (BATCH, SEQ, D_MODEL), 1.0),
        "x_sample": nrm(ks[1], (DEC_BATCH, DEC_SEQ, D_MODEL), 1.0),
        "state_pool": nrm(ks[2], (DEPTH, DEC_BATCH, POOL_BUF, D_POOL), 1.0),
        "state_sconv": nrm(ks[3], (DEPTH, DEC_BATCH, SCONV_K - 1, D_SCONV), 1.0),
        "state_cconv": nrm(ks[4], (DEPTH, DEC_BATCH, CCONV_K - 1, D_CCONV), 0.5),
        "state_ffn": nrm(ks[5], (DEPTH, DEC_BATCH, FFN_CONV_K - 1, 2 * D_FF), 1.0),
        "norm1_g": 1.0 + nrm(ks[6], (DEPTH, D_MODEL), 0.02),
        "w_in": nrm(ks[7], (DEPTH, D_MODEL, D_IN), D_MODEL ** -0.5),
        "pool_w": nrm(ks[8], (DEPTH, N_POOL_GROUPS, POOL_GROUP, POOL_GROUP), POOL_GROUP ** -0.5),
        "pool_scale": 1.0 + nrm(ks[9], (DEPTH, D_POOL), 0.02),
        "sconv_w": nrm(ks[10], (DEPTH, SCONV_K, D_SCONV), SCONV_K ** -0.5),
        "cconv_w": nrm(ks[11], (DEPTH, CCONV_K, D_CCONV), CCONV_K ** -0.5),
        "cconv_b": nrm(ks[12], (DEPTH, D_CCONV), 0.02),
        "cconv_norm_g": 1.0 + nrm(ks[13], (DEPTH, D_CCONV), 0.02),
        "w_out": nrm(ks[14], (DEPTH, D_MODEL, D_MODEL), D_MODEL ** -0.5),
        "norm2_g": 1.0 + nrm(ks[15], (DEPTH, D_MODEL), 0.02),
        "w_up": nrm(ks[16], (DEPTH, D_MODEL, 2 * D_FF), D_MODEL ** -0.5),
        "ffn_conv_w": nrm(ks[17], (DEPTH, FFN_CONV_K, 2 * D_FF), FFN_CONV_K ** -0.5),
        "w_down": nrm(ks[18], (DEPTH, D_FF, D_MODEL), D_FF ** -0.5),
        "final_norm_g": 1.0 + nrm(ks[19], (D_MODEL,), 0.02),
    }


def reference(x_prompt, x_sample, state_pool, state_sconv, state_cconv, state_ffn,
              norm1_g, w_in, pool_w, pool_scale, sconv_w, cconv_w, cconv_b, cconv_norm_g,
              w_out, norm2_g, w_up, ffn_conv_w, w_down, final_norm_g):
    dt_p = x_prompt.dtype
    zp_pool = jnp.zeros((BATCH, POOL_BUF, D_POOL), dt_p)
    zp_sconv = jnp.zeros((BATCH, SCONV_K - 1, D_SCONV), dt_p)
    zp_cconv = jnp.zeros((BATCH, CCONV_K - 1, D_CCONV), dt_p)
    zp_ffn = jnp.zeros((BATCH, FFN_CONV_K - 1, 2 * D_FF), dt_p)

    xp, xs = x_prompt, x_sample
    pp, sp, cp, fp = [], [], [], []
    ps, ss, cs_, fs = [], [], [], []
    for i in range(DEPTH):
        layer_w = (norm1_g[i], w_in[i], pool_w[i], pool_scale[i], sconv_w[i], cconv_w[i], cconv_b[i],
                   cconv_norm_g[i], w_out[i], norm2_g[i], w_up[i], ffn_conv_w[i], w_down[i])
        xp, a, b, c, d = hybrid_layer(xp, zp_pool, zp_sconv, zp_cconv, zp_ffn, 0, *layer_w)
        pp.append(a); sp.append(b); cp.append(c); fp.append(d)
        xs, a, b, c, d = hybrid_layer(xs, state_pool[i], state_sconv[i], state_cconv[i], state_ffn[i],
                                      PAST_LEN, *layer_w)
        ps.append(a); ss.append(b); cs_.append(c); fs.append(d)

    y_prompt = rmsnorm(xp, final_norm_g)
    y_sample = rmsnorm(xs, final_norm_g)
    new_pool_p = jnp.stack(pp, axis=0)
    new_sconv_p = jnp.stack(sp, axis=0)
    new_cconv_p = jnp.stack(cp, axis=0)
    new_ffn_p = jnp.stack(fp, axis=0)
    new_pool_s = jnp.stack(ps, axis=0)
    new_sconv_s = jnp.stack(ss, axis=0)
    new_cconv_s = jnp.stack(cs_, axis=0)
    new_ffn_s = jnp.stack(fs, axis=0)
    return (y_prompt, y_sample, new_pool_p, new_sconv_p, new_cconv_p, new_ffn_p,
            new_pool_s, new_sconv_s, new_cconv_s, new_ffn_s)
```

```python
from contextlib import ExitStack
import numpy as np
import concourse.bass as bass
import concourse.mybir as mybir
from concourse.bass_utils import run_bass_kernel_spmd

F32 = mybir.dt.float32
BF16 = mybir.dt.bfloat16
AF = mybir.ActivationFunctionType
ALU = mybir.AluOpType

D = 2048
DEPTH = 2
SEQ = 2048
D_POOL, D_SC, D_CC = 512, 768, 768
D_FF = 5632
D_IN = 4352
EPS = 1e-6
NP = 1056
NS = 128
T = NP + NS
BLKS = [(0, 512), (512, 512), (1024, 160)]
NCORES = 8
WSLOTS = 5
GF = 6
ENGS = ("pe", "act", "dve", "pool", "sp")


class Op:
    __slots__ = ("eng", "fn", "deps", "idx", "milestone", "mnum", "is_dma",
                 "dma_sem", "dma_val", "pre_dma_wait")

    def __init__(self, eng, fn):
        self.eng = eng
        self.fn = fn
        self.deps = []
        self.idx = -1
        self.milestone = False
        self.mnum = 0
        self.is_dma = False
        self.dma_sem = None
        self.dma_val = 0
        self.pre_dma_wait = None


class Sched:
    def __init__(self, n_dma_sems=12):
        self.ops = {e: [] for e in ENGS}
        self.res_w = {}
        self.res_r = {}
        self.n_dma_sems = n_dma_sems
        self.dma_count = {e: 0 for e in ENGS}

    def op(self, eng, fn, reads=(), writes=(), dma=False):
        o = Op(eng, fn)
        best = {}
        dmadeps = {}

        def add(d):
            if d is None:
                return
            if d.is_dma:
                dmadeps[id(d)] = d
            else:
                b = best.get(d.eng)
                if b is None or d.idx > b.idx:
                    best[d.eng] = d

        for r in reads:
            add(self.res_w.get(r))
        for r in writes:
            add(self.res_w.get(r))
            rr = self.res_r.get(r)
            if rr:
                for rd in rr.values():
                    add(rd)
        o.deps = list(best.values()) + list(dmadeps.values())
        o.idx = len(self.ops[eng])
        o.is_dma = dma
        for r in reads:
            rr = self.res_r.setdefault(r, {})
            rr[("d", id(o)) if dma else eng] = o
        for r in writes:
            self.res_w[r] = o
            self.res_r[r] = {}
        self.ops[eng].append(o)
        if dma:
            n = self.dma_count[eng]
            self.dma_count[eng] = n + 1
            k = n % self.n_dma_sems
            o.dma_sem = (eng, k)
            o.dma_val = 16 * (n // self.n_dma_sems + 1)
            if n >= self.n_dma_sems:
                o.pre_dma_wait = ((eng, k), 16 * (n // self.n_dma_sems))
        return o

    def emit(self, block_fns, sems, dma_sems):
        plans = {}
        for e in ENGS:
            known = {f: -1 for f in ENGS}
            known_dma = {}
            plan = []
            for o in self.ops[e]:
                waits = []
                if o.pre_dma_wait is not None:
                    s, v = o.pre_dma_wait
                    if known_dma.get(s, 0) < v:
                        known_dma[s] = v
                        waits.append(("dma", s, v))
                for d in o.deps:
                    if d.is_dma:
                        if known_dma.get(d.dma_sem, 0) < d.dma_val:
                            known_dma[d.dma_sem] = d.dma_val
                            waits.append(("dma", d.dma_sem, d.dma_val))
                    else:
                        if d.eng == e and e == "pe":
                            continue
                        if known[d.eng] >= d.idx:
                            continue
                        known[d.eng] = d.idx
                        d.milestone = True
                        waits.append(("eng", d))
                plan.append(waits)
            plans[e] = plan
        for e in ENGS:
            m = 0
            for o in self.ops[e]:
                if o.milestone:
                    m += 1
                    o.mnum = m
        final_dma = {}
        for e in ENGS:
            for o in self.ops[e]:
                if o.is_dma:
                    final_dma[o.dma_sem] = max(final_dma.get(o.dma_sem, 0), o.dma_val)

        def make(e):
            def body(eng):
                for o, waits in zip(self.ops[e], plans[e]):
                    for w in waits:
                        if w[0] == "dma":
                            eng.wait_ge(dma_sems[w[1]], w[2])
                        else:
                            eng.wait_ge(sems[w[1].eng], w[1].mnum)
                    ins = o.fn(eng)
                    if o.is_dma:
                        ins.then_inc(dma_sems[o.dma_sem], 16)
                    elif o.milestone:
                        ins.then_inc(sems[e], 1)
                if e == "sp":
                    for s, v in sorted(final_dma.items()):
                        eng.wait_ge(dma_sems[s], v)
            return body

        for e in ENGS:
            block_fns[e](make(e))


def _vec_layout():
    off = {}
    n = 0

    def add(name, c):
        nonlocal n
        off[name] = n
        n += c

    for l in range(DEPTH):
        add(("n1g", l), 16)
        add(("n2g", l), 16)
        add(("psc", l), 4)
        for t in range(3):
            add(("sw", l, t), 6)
        for t in range(31):
            add(("cw", l, t), 6)
        add(("cb", l), 6)
        add(("cg", l), 6)
        for t in range(3):
            add(("fw", l, t), 88)
    add(("fng",), 16)
    return off, n


VOFF, NV = _vec_layout()


def _pack_vecs(inp):
    v = np.zeros((128, NV), np.float32)

    def put(name, arr):
        a = np.asarray(arr, np.float32).reshape(-1, 128).T
        v[:, VOFF[name]:VOFF[name] + a.shape[1]] = a

    for l in range(DEPTH):
        put(("n1g", l), inp["norm1_g"][l])
        put(("n2g", l), inp["norm2_g"][l])
        put(("psc", l), inp["pool_scale"][l])
        for t in range(3):
            put(("sw", l, t), inp["sconv_w"][l, t])
        for t in range(31):
            put(("cw", l, t), inp["cconv_w"][l, t])
        put(("cb", l), inp["cconv_b"][l])
        put(("cg", l), inp["cconv_norm_g"][l])
        for t in range(3):
            put(("fw", l, t), inp["ffn_conv_w"][l, t])
    put(("fng",), inp["final_norm_g"])
    return v


_NC_CACHE = {}


def build_program():
    nc = bass.Bass("TRN2", target_bir_lowering=False)

    def din(name, shape):
        return nc.dram_tensor(name, list(shape), F32, kind="ExternalInput").ap()

    def dout(name, shape):
        return nc.dram_tensor(name, list(shape), F32, kind="ExternalOutput").ap()

    xp_d = din("xp", [NP, D])
    xs_d = din("xs", [NS, D])
    stp_d = din("st_pool", [DEPTH, 240, D_POOL])
    sts_d = din("st_sconv", [DEPTH, 32, D_SC])
    stc_d = din("st_cconv", [DEPTH, 480, D_CC])
    stf_d = din("st_ffn", [DEPTH, 32, 2 * D_FF])
    vecs_d = din("vecs", [128, NV])
    cnt_d = din("cnt", [128, 4 * 16])
    ident_d = din("ident", [128, 128])
    w_in_d = din("w_in", [DEPTH, D, D_IN])
    pool_w_d = din("pool_w", [DEPTH, 4, 128, 128])
    w_out_d = din("w_out", [DEPTH, D, D])
    w_up_d = din("w_up", [DEPTH, D, 2 * D_FF])
    w_down_d = din("w_down", [DEPTH, D_FF, D])

    yp_d = dout("yp", [NP, D])
    ys_d = dout("ys", [NS, D])
    o_pool_p = dout("o_pool_p", [DEPTH, 15, D_POOL])
    o_sconv_p = dout("o_sconv_p", [DEPTH, 2, D_SC])
    o_cconv_p = dout("o_cconv_p", [DEPTH, 30, D_CC])
    o_ffn_p = dout("o_ffn_p", [DEPTH, 2, 2 * D_FF])
    o_pool_s = dout("o_pool_s", [DEPTH, 240, D_POOL])
    o_sconv_s = dout("o_sconv_s", [DEPTH, 32, D_SC])
    o_cconv_s = dout("o_cconv_s", [DEPTH, 480, D_CC])
    o_ffn_s = dout("o_ffn_s", [DEPTH, 32, 2 * D_FF])

    S = Sched()
    es = ExitStack()

    def sb(name, shape, dt):
        return es.enter_context(nc.sbuf_tensor("sb_" + name, shape, dt))

    xT = sb("xT", [128, 16, T], F32)
    hT = sb("hT", [128, 16, T], BF16)
    NHALF = 2 * WSLOTS
    wr = sb("wr", [128, NHALF * 8, 128], BF16)
    vecs = sb("vecs", [128, NV], F32)
    rstd = sb("rstd", [128, T], F32)
    sqb = [sb(f"sq{i}", [128, T], BF16) for i in range(2)]
    identF = sb("identF", [128, 128], F32)
    onesB = sb("onesB", [128, 128], BF16)
    epsT = sb("epsT", [128, 1], F32)
    identB = sb("identB", [128, 128], BF16)
    dg = sb("dg", [128, 4, 128], BF16)
    dg_rot = {"i": 0}
    cnt = sb("cnt", [128, 4, 16], F32)
    poolw = sb("poolw", [128, 4, 128], BF16)
    CC_W = 6 * T
    EW = 1704
    ARENA_W = CC_W + 3 * EW + 2 * T + 512 + 256
    arena = sb("arena", [128, ARENA_W], F32)
    cc = arena[:, 0:CC_W]
    cc_bf = cc.bitcast(BF16)
    Ebuf = [arena[:, CC_W + k * EW: CC_W + (k + 1) * EW] for k in range(3)]
    o_t = CC_W + 3 * EW
    tbuf = [arena[:, o_t + k * T: o_t + (k + 1) * T] for k in range(2)]
    o_s = o_t + 2 * T
    stg = arena[:, o_s:o_s + 512]
    tailstg = arena[:, o_s + 512:o_s + 768]

    pA = es.enter_context(nc.psum_tensor("pA", [128, 1536], F32))
    pB = es.enter_context(nc.psum_tensor("pB", [128, 1536], F32))
    pC = es.enter_context(nc.psum_tensor("pC", [128, 1024], F32))
    PS = {"pA": pA, "pB": pB}

    sems = {e: es.enter_context(nc.semaphore(f"s_{e}")) for e in ENGS}
    dsems = {}
    for e in ("sp", "pool", "act"):
        for k in range(S.n_dma_sems):
            dsems[(e, k)] = es.enter_context(nc.semaphore(f"d_{e}{k}"))
    block = es.enter_context(nc.Block())

    def V(name, c=0, n=1):
        o = VOFF[name] + c
        return vecs[:, o:o + n]

    def ccres_bf(m):
        return ("cc", m // 2)

    def mixbf(m):
        return cc_bf[:, m * T:(m + 1) * T]

    ps_state = {"i": 0}

    def next_ps():
        n = ("pA", "pB")[ps_state["i"] % 2]
        ps_state["i"] += 1
        return n

    evac_state = {"i": 0}

    wstate = {"i": 0}

    def wload(src_ap, nk):
        p = wstate["i"]
        if nk > 8:
            if p % 2 == 1:
                p += 1
            hs = [p % NHALF, (p + 1) % NHALF]
            wstate["i"] = p + 2
        else:
            hs = [p % NHALF]
            wstate["i"] = p + 1
        base = hs[0] * 8
        res = [("w", h) for h in hs]
        S.op("pool", lambda e: e.dma_start(out=wr[:, base:base + nk, :], in_=src_ap),
             writes=res, dma=True)
        return base, res

    def wcols(wd, l, rows0, nk, col0):
        return wd[l, rows0:rows0 + nk * 128, col0:col0 + 128].rearrange("(kc p) c -> p kc c", p=128)

    def mm_group(psn, lhs_fn, rhs_fn, nk, reads, kreads=None):
        ps = PS[psn]
        for k in range(nk):
            rk = reads if kreads is None else reads + kreads(k)
            for (c0, n) in BLKS:
                S.op("pe", lambda e, k=k, c0=c0, n=n: e.matmul(
                    ps[:, c0:c0 + n], lhsT=lhs_fn(k), rhs=rhs_fn(k)[:, c0:c0 + n],
                    start=(k == 0), stop=(k == nk - 1)),
                    reads=rk, writes=[psn])

    def proj_chunk(wd, l, col0, src_fn, src_res, nk=16, rows0=0):
        base, wres = wload(wcols(wd, l, rows0, nk, col0), nk)
        psn = next_ps()
        pend = deferred[:]
        deferred.clear()
        if callable(src_res):
            mm_group(psn, lambda k: wr[:, base + k, :], src_fn, nk, reads=wres, kreads=src_res)
        else:
            mm_group(psn, lambda k: wr[:, base + k, :], src_fn, nk, reads=wres + src_res)
        for f in pend:
            f()
        return psn

    deferred = []

    def flush_deferred():
        pend = deferred[:]
        deferred.clear()
        for f in pend:
            f()

    def norm_stats(src_fn, src_res, nch, inv_n, bias_fn=None):
        psn = next_ps()
        ps = PS[psn]
        for c in range(nch):
            sq = sqb[c % 2]
            sqr = ("sq", c % 2)
            if bias_fn is None and c % 2 == 1:
                S.op("dve", lambda e, c=c, sq=sq: e.tensor_tensor(out=sq[:], in0=src_fn(c), in1=src_fn(c), op=ALU.mult),
                     reads=src_res(c), writes=[sqr])
            elif bias_fn is None:
                S.op("act", lambda e, c=c, sq=sq: e.activation(out=sq[:], in_=src_fn(c), func=AF.Square),
                     reads=src_res(c), writes=[sqr])
            else:
                S.op("act", lambda e, c=c, sq=sq: e.activation(out=sq[:], in_=src_fn(c), func=AF.Square, bias=bias_fn(c)),
                     reads=src_res(c), writes=[sqr])
            for (c0, n) in BLKS:
                S.op("pe", lambda e, c=c, c0=c0, n=n, sq=sq: e.matmul(
                    ps[:, c0:c0 + n], lhsT=onesB[:], rhs=sq[:, c0:c0 + n],
                    start=(c == 0), stop=(c == nch - 1)),
                    reads=[sqr, "onesB"], writes=[psn])
        S.op("act", lambda e: e.activation(out=rstd[:], in_=ps[:, 0:T], func=AF.Sqrt, bias=epsT[:], scale=inv_n),
             reads=[psn, "epsT"], writes=["rstd"])
        S.op("dve", lambda e: e.reciprocal(out=rstd[:], in_=rstd[:]),
             reads=["rstd"], writes=["rstd"])

    def rmsnorm_to_h(gname):
        norm_stats(lambda c: xT[:, c, :], lambda c: [("x", c)], 16, 1.0 / D)
        for c in range(16):
            S.op("dve", lambda e, c=c: e.scalar_tensor_tensor(
                out=hT[:, c, :], in0=xT[:, c, :], scalar=V(gname, c), in1=rstd[:],
                op0=ALU.mult, op1=ALU.mult),
                reads=[("x", c), "rstd", "vecs"], writes=[("h", c)])

    H_RES = lambda k: [("h", k)]

    cstate = {"i": 0}

    def next_c():
        k = cstate["i"] % 2
        cstate["i"] += 1
        return k

    def load_state_T(src_rows_fn, nrows_total, dst_ap_fn):
        ntile = (nrows_total + 119) // 120
        rows = []
        r0 = 0
        for i in range(ntile):
            n = min(120, nrows_total - r0)
            rows.append((r0, n))
            r0 += n
        ck = next_c()
        cres = ("pC", ck)
        if ntile == 1:
            tl = [stg_rot["i"] % 4]
            stg_rot["i"] += 1
        else:
            tl = list(range(ntile))
        for i, (r0, n) in enumerate(rows):
            ti = tl[i]
            S.op("sp", lambda e, ti=ti, r0=r0, n=n: e.dma_start(out=stg[0:n, ti * 128:(ti + 1) * 128], in_=src_rows_fn(r0, n)),
                 writes=[("stg", ti)], dma=True)
        for i, (r0, n) in enumerate(rows):
            ti = tl[i]
            S.op("pe", lambda e, ti=ti, r0=r0, n=n: e.transpose(
                out=pC[:, ck * 512 + r0: ck * 512 + r0 + n], in_=stg[0:n, ti * 128:(ti + 1) * 128],
                identity=identF[0:n, 0:n]),
                reads=[("stg", ti), "identF"], writes=[cres])
        return ck, cres

    stg_rot = {"i": 0}

    tstate = {"i": 0}

    def store_tail(src_ap, ncols, src_res, dst_ap):
        deferred.append(lambda: _store_tail(src_ap, ncols, src_res, dst_ap))

    def _store_tail(src_ap, ncols, src_res, dst_ap):
        ck = next_c()
        cres = ("pC", ck)
        S.op("pe", lambda e: e.transpose(out=pC[0:ncols, ck * 512: ck * 512 + 128], in_=src_ap, identity=identF[:]),
             reads=list(src_res) + ["identF"], writes=[cres])
        k = tstate["i"] % 2
        tstate["i"] += 1
        ts = tailstg[:, k * 128:(k + 1) * 128]
        S.op("act", lambda e: e.activation(out=ts[0:ncols, :], in_=pC[0:ncols, ck * 512: ck * 512 + 128], func=AF.Copy),
             reads=[cres], writes=[("tail", k)])
        S.op("act", lambda e: e.dma_start(out=dst_ap, in_=ts[0:ncols, :]), reads=[("tail", k)], dma=True)

    S.op("sp", lambda e: e.dma_start(out=vecs[:], in_=vecs_d), writes=["vecs"], dma=True)
    S.op("sp", lambda e: e.dma_start(out=identF[:], in_=ident_d), writes=["identF"], dma=True)
    S.op("sp", lambda e: e.dma_start(out=cnt[:], in_=cnt_d.rearrange("p (a b) -> p a b", a=4)), writes=["cnt"], dma=True)
    S.op("dve", lambda e: e.memset(onesB[:], 1.0), writes=["onesB"])
    S.op("dve", lambda e: e.memset(epsT[:], EPS), writes=["epsT"])
    S.op("dve", lambda e: e.tensor_copy(out=identB[:], in_=identF[:]), reads=["identF"], writes=["identB"])
    for k in range(3):
        S.op("dve", lambda e, k=k: e.memset(Ebuf[k], 0.0), writes=[("E", k)])

    xin = [arena[:, 0:2048], arena[:, 2368:2368 + 2048], arena[:, 4736:4736 + 2048]]
    xin_res = [[("cc", 0), ("cc", 1)], [("cc", 2), ("cc", 3)], [("cc", 4), ("cc", 5)]]
    tiles = [(xp_d, i * 128, 128, i * 128) for i in range(8)] + [(xp_d, 1024, 32, 1024), (xs_d, 0, 128, NP)]
    for ti, (src, r0, n, col0) in enumerate(tiles):
        xi = xin[ti % 3]
        xr = xin_res[ti % 3]
        S.op("sp", lambda e, xi=xi, src=src, r0=r0, n=n: e.dma_start(out=xi[0:n, :], in_=src[r0:r0 + n, :]),
             writes=xr, dma=True)
        for half in range(2):
            psn = next_ps()
            ps = PS[psn]
            for j in range(8):
                kc = half * 8 + j
                S.op("pe", lambda e, ps=ps, j=j, kc=kc, xi=xi, n=n: e.transpose(
                    out=ps[:, j * 128: j * 128 + n], in_=xi[0:n, kc * 128:(kc + 1) * 128], identity=identF[0:n, 0:n]),
                    reads=xr + ["identF"], writes=[psn])
            src_v = ps[:, 0:1024].rearrange("p (a b) -> p a b", a=8)[:, :, 0:n]
            dst_v = xT[:, half * 8:(half + 1) * 8, col0:col0 + n]
            xres = [("x", c) for c in range(half * 8, half * 8 + 8)]
            if half == 0:
                S.op("act", lambda e, dst_v=dst_v, src_v=src_v: e.activation(out=dst_v, in_=src_v, func=AF.Copy),
                     reads=[psn], writes=xres)
            else:
                S.op("dve", lambda e, dst_v=dst_v, src_v=src_v: e.tensor_copy(out=dst_v, in_=src_v),
                     reads=[psn], writes=xres)

    def conv_taps(dst, ext, p_off, s_off, wname, l, ci, ntap, ext_res, dst_res, bias=None):
        for (d0, dn, e0, sh) in ((0, NP, p_off, 1), (NP, NS, s_off, 16)):
            for i in range(ntap):
                src = ext[:, e0 + i * sh: e0 + i * sh + dn]
                w = V((wname, l, i), ci)
                if i == 0:
                    if bias is None:
                        S.op("dve", lambda e, src=src, w=w, d0=d0, dn=dn: e.tensor_scalar(
                            out=dst[:, d0:d0 + dn], in0=src, scalar1=w, scalar2=None, op0=ALU.mult),
                            reads=ext_res + ["vecs"], writes=dst_res)
                    else:
                        S.op("dve", lambda e, src=src, w=w, d0=d0, dn=dn: e.tensor_scalar(
                            out=dst[:, d0:d0 + dn], in0=src, scalar1=w, scalar2=bias, op0=ALU.mult, op1=ALU.add),
                            reads=ext_res + ["vecs"], writes=dst_res)
                else:
                    S.op("dve", lambda e, src=src, w=w, d0=d0, dn=dn: e.scalar_tensor_tensor(
                        out=dst[:, d0:d0 + dn], in0=src, scalar=w, in1=dst[:, d0:d0 + dn],
                        op0=ALU.mult, op1=ALU.add),
                        reads=ext_res + ["vecs"] + dst_res, writes=dst_res)

    def out_proj_partial(l, mix_fn, mix_res, kc0, nk):
        for dc in range(16):
            psn = proj_chunk(w_out_d, l, dc * 128, mix_fn, mix_res, nk=nk, rows0=kc0 * 128)
            ps = PS[psn]
            S.op("dve", lambda e, ps=ps, dc=dc: e.tensor_tensor(out=xT[:, dc, :], in0=ps[:, 0:T], in1=xT[:, dc, :], op=ALU.add),
                 reads=[psn, ("x", dc)], writes=[("x", dc)])

    def zero_halos():
        for k in range(3):
            S.op("dve", lambda e, k=k: e.memset(Ebuf[k][:, 0:32], 0.0), writes=[("E", k), ("Es", k)])

    ES0 = 1218
    stT = [Ebuf[0][:, ES0:ES0 + 384].rearrange("p (a b) -> p a b", b=32), Ebuf[1][:, ES0:ES0 + 384].rearrange("p (a b) -> p a b", b=32)]
    oT3 = Ebuf[2][:, ES0:ES0 + 384].rearrange("p (a b) -> p a b", b=32)
    oP3 = stg[:, 0:384].rearrange("p (a b) -> p a b", b=32)
    STG_ALL = [("stg", i) for i in range(4)]

    for l in range(DEPTH):
        rmsnorm_to_h(("n1g", l))
        S.op("pool", lambda e, l=l: e.dma_start(out=poolw[:], in_=pool_w_d[l].rearrange("g c d -> c g d")),
             writes=["poolw"], dma=True)

        P_OFF, PS_OFF = 15, 15 + NP
        zero_halos()
        for g in range(4):
            kwin = (2, 4, 8, 16)[g]
            E0, E1, E2 = Ebuf[0], Ebuf[1], Ebuf[2]
            psn = proj_chunk(w_in_d, l, g * 128, lambda k: hT[:, k, :], H_RES)
            ps = PS[psn]
            ck, cres = load_state_T(lambda r0, n, g=g, l=l: stp_d[l, r0:r0 + n, g * 128:(g + 1) * 128], 240, None)
            S.op("act", lambda e, ck=ck: e.activation(out=E0[:, PS_OFF:PS_OFF + 240], in_=pC[:, ck * 512: ck * 512 + 240], func=AF.Copy),
                 reads=[cres], writes=[("E", 0)])
            S.op("act", lambda e, ps=ps: e.activation(out=E0[:, P_OFF:P_OFF + NP], in_=ps[:, 0:NP], func=AF.Copy),
                 reads=[psn], writes=[("E", 0)])
            S.op("act", lambda e, ps=ps: e.activation(out=E0[:, PS_OFF + 240:PS_OFF + 368], in_=ps[:, NP:T], func=AF.Copy),
                 reads=[psn], writes=[("E", 0)])
            store_tail(E0[:, P_OFF + NP - 15:P_OFF + NP], 15, [("E", 0)], o_pool_p[l, :, g * 128:(g + 1) * 128])
            store_tail(E0[:, PS_OFF + 240:PS_OFF + 368], 128, [("E", 0)], o_pool_s[l, 7 * 16:15 * 16, g * 128:(g + 1) * 128])
            cur, cur_r = E0, ("E", 0)
            others = [(E1, ("E", 1)), (E2, ("E", 2))]
            sh = 1
            step = 0
            while sh < kwin:
                nxt, nxt_r = others[step % 2]
                S.op("dve", lambda e, cur=cur, nxt=nxt, sh=sh: e.tensor_tensor(
                    out=nxt[:, sh:P_OFF + NP], in0=cur[:, sh:P_OFF + NP], in1=cur[:, 0:P_OFF + NP - sh], op=ALU.add),
                    reads=[cur_r], writes=[nxt_r])
                S.op("dve", lambda e, cur=cur, nxt=nxt, sh=sh: e.tensor_tensor(
                    out=nxt[:, PS_OFF + sh * 16:PS_OFF + 368], in0=cur[:, PS_OFF + sh * 16:PS_OFF + 368],
                    in1=cur[:, PS_OFF:PS_OFF + 368 - sh * 16], op=ALU.add),
                    reads=[cur_r], writes=[nxt_r])
                cur, cur_r = nxt, nxt_r
                sh *= 2
                step += 1
            dbf = tbuf[0].bitcast(BF16)[:, (g % 2) * T:(g % 2 + 1) * T]
            tdr = ("td", g % 2)
            inv = 1.0 / kwin
            S.op("dve", lambda e, cur=cur, inv=inv, dbf=dbf: e.scalar_tensor_tensor(
                out=dbf[:, 0:NP], in0=cur[:, P_OFF:P_OFF + NP], scalar=inv, in1=E0[:, P_OFF:P_OFF + NP],
                op0=ALU.mult, op1=ALU.subtract),
                reads=[cur_r, ("E", 0)], writes=[tdr])
            S.op("dve", lambda e, cur=cur, inv=inv, dbf=dbf: e.scalar_tensor_tensor(
                out=dbf[:, NP:T], in0=cur[:, PS_OFF + 240:PS_OFF + 368], scalar=inv, in1=E0[:, PS_OFF + 240:PS_OFF + 368],
                op0=ALU.mult, op1=ALU.subtract),
                reads=[cur_r, ("E", 0)], writes=[tdr])
            f16 = tbuf[1][:, 0:16]
            S.op("dve", lambda e, cur=cur, g=g: e.tensor_tensor(out=f16, in0=cur[:, P_OFF:P_OFF + 16], in1=cnt[:, g, :], op=ALU.mult),
                 reads=[cur_r, "cnt"], writes=[("t", 1)])
            S.op("dve", lambda e, dbf=dbf: e.tensor_tensor(out=dbf[:, 0:16], in0=f16, in1=E0[:, P_OFF:P_OFF + 16], op=ALU.subtract),
                 reads=[("t", 1), ("E", 0), tdr], writes=[tdr])

            def pool_mm(g=g, l=l, dbf=dbf, tdr=tdr):
                psn2 = next_ps()
                ps2 = PS[psn2]
                for (c0, n) in BLKS:
                    S.op("pe", lambda e, ps2=ps2, c0=c0, n=n, g=g, dbf=dbf: e.matmul(ps2[:, c0:c0 + n], lhsT=poolw[:, g, :], rhs=dbf[:, c0:c0 + n], start=True, stop=True),
                         reads=["poolw", tdr], writes=[psn2])
                S.op("act", lambda e, ps2=ps2, g=g, l=l: e.activation(out=mixbf(g), in_=ps2[:, 0:T], func=AF.Copy, scale=V(("psc", l), g)),
                     reads=[psn2, "vecs"], writes=[ccres_bf(g)])
            deferred.append(pool_mm)
        flush_deferred()
        out_proj_partial(l, lambda k: mixbf(k), lambda k: [ccres_bf(k)], 0, 4)
        zero_halos()

        o1 = D_POOL
        SP_OFF, SS_OFF = 2, 2 + NP
        for i in range(6):
            Ei, Er = Ebuf[i % 2], ("E", i % 2)
            gc, gcr = tbuf[0], ("t", 0)
            cbuf, cbr = tbuf[1], ("t", 1)
            psn = proj_chunk(w_in_d, l, o1 + D_SC + i * 128, lambda k: hT[:, k, :], H_RES)
            ps = PS[psn]
            ck, cres = load_state_T(lambda r0, n, i=i, l=l: sts_d[l, r0:r0 + n, i * 128:(i + 1) * 128], 32, None)
            S.op("act", lambda e, ck=ck, Ei=Ei: e.activation(out=Ei[:, SS_OFF:SS_OFF + 32], in_=pC[:, ck * 512: ck * 512 + 32], func=AF.Copy),
                 reads=[cres], writes=[Er])
            S.op("act", lambda e, ps=ps: e.activation(out=gc[:, 0:T], in_=ps[:, 0:T], func=AF.Copy),
                 reads=[psn], writes=[gcr])
            psn = proj_chunk(w_in_d, l, o1 + 2 * D_SC + i * 128, lambda k: hT[:, k, :], H_RES)
            ps = PS[psn]
            S.op("dve", lambda e, ps=ps, Ei=Ei: e.tensor_tensor(out=Ei[:, SP_OFF:SP_OFF + NP], in0=ps[:, 0:NP], in1=gc[:, 0:NP], op=ALU.mult),
                 reads=[psn, gcr], writes=[Er])
            S.op("dve", lambda e, ps=ps, Ei=Ei: e.tensor_tensor(out=Ei[:, SS_OFF + 32:SS_OFF + 160], in0=ps[:, NP:T], in1=gc[:, NP:T], op=ALU.mult),
                 reads=[psn, gcr], writes=[Er])
            store_tail(Ei[:, SP_OFF + NP - 2:SP_OFF + NP], 2, [Er], o_sconv_p[l, :, i * 128:(i + 1) * 128])
            store_tail(Ei[:, SS_OFF + 32 + 96:SS_OFF + 160], 32, [Er], o_sconv_s[l, :, i * 128:(i + 1) * 128])
            conv_taps(cbuf, Ei, 0, SS_OFF, "sw", l, i, 3, [Er], [cbr])
            psn = proj_chunk(w_in_d, l, o1 + i * 128, lambda k: hT[:, k, :], H_RES)
            ps = PS[psn]
            S.op("dve", lambda e, ps=ps, i=i: e.tensor_tensor(out=mixbf(i), in0=ps[:, 0:T], in1=cbuf[:, 0:T], op=ALU.mult),
                 reads=[psn, cbr], writes=[ccres_bf(i)])
        out_proj_partial(l, lambda k: mixbf(k), lambda k: [ccres_bf(k)], 4, 6)

        o4 = D_POOL + 3 * D_SC
        CP_OFF, CS_OFF = 30, 30 + NP
        zero_halos()
        VT_P, VS = 1000, 1040
        pending_conv = []

        def cconv_pe(i, Ei, Er, l=l):
            Ebf = Ei.bitcast(BF16)
            psn = next_ps()
            ps = PS[psn]
            for t in range(31):
                r = dg_rot["i"] % 4
                dg_rot["i"] += 1
                S.op("dve", lambda e, r=r, t=t: e.tensor_scalar(out=dg[:, r, :], in0=identB[:], scalar1=V(("cw", l, t), i), scalar2=None, op0=ALU.mult),
                     reads=["identB", "vecs"], writes=[("dg", r)])
                for (c0, n, src0) in ((0, 512, t), (512, 512, t + 512), (1024, 32, t + 1024)):
                    S.op("pe", lambda e, r=r, t=t, c0=c0, n=n, src0=src0: e.matmul(
                        ps[:, c0:c0 + n], lhsT=dg[:, r, :], rhs=Ebf[:, src0:src0 + n], start=(t == 0), stop=(t == 30)),
                        reads=[("dg", r), Er], writes=[psn])
            S.op("act", lambda e: e.activation(out=cc[:, i * T:i * T + NP], in_=ps[:, 0:NP], func=AF.Identity, bias=V(("cb", l), i)),
                 reads=[psn, "vecs"], writes=[("cc", i)])

        for i in range(6):
            Ei, Er = Ebuf[i % 2], ("E", i % 2)
            Ebf = Ei.bitcast(BF16)
            sg, sgr = tbuf[0], ("t", 0)
            psn = proj_chunk(w_in_d, l, o4 + D_CC + i * 128, lambda k: hT[:, k, :], H_RES)
            ps = PS[psn]
            ck, cres = load_state_T(lambda r0, n, i=i, l=l: stc_d[l, r0:r0 + n, i * 128:(i + 1) * 128], 480, None)
            S.op("act", lambda e, ck=ck, Ei=Ei: e.activation(out=Ei[:, VS:VS + 480], in_=pC[:, ck * 512: ck * 512 + 480], func=AF.Copy),
                 reads=[cres], writes=[Er])
            for f in pending_conv:
                f()
            pending_conv.clear()
            S.op("act", lambda e, ps=ps: e.activation(out=sg[:, 0:T], in_=ps[:, 0:T], func=AF.Sigmoid),
                 reads=[psn], writes=[sgr])
            psn = proj_chunk(w_in_d, l, o4 + i * 128, lambda k: hT[:, k, :], H_RES)
            ps = PS[psn]
            S.op("dve", lambda e, ps=ps, Ebf=Ebf: e.tensor_tensor(out=Ebf[:, CP_OFF:CP_OFF + NP], in0=ps[:, 0:NP], in1=sg[:, 0:NP], op=ALU.mult),
                 reads=[psn, sgr], writes=[Er])
            S.op("dve", lambda e, ps=ps, Ei=Ei: e.tensor_tensor(out=Ei[:, VS + 480:VS + 608], in0=ps[:, NP:T], in1=sg[:, NP:T], op=ALU.mult),
                 reads=[psn, sgr], writes=[Er])
            S.op("dve", lambda e, ps=ps, Ei=Ei: e.tensor_tensor(out=Ei[:, VT_P:VT_P + 30], in0=ps[:, NP - 30:NP], in1=sg[:, NP - 30:NP], op=ALU.mult),
                 reads=[psn, sgr], writes=[Er])
            store_tail(Ei[:, VT_P:VT_P + 30], 30, [Er], o_cconv_p[l, :, i * 128:(i + 1) * 128])
            store_tail(Ei[:, VS + 480:VS + 608], 128, [Er], o_cconv_s[l, 22 * 16:30 * 16, i * 128:(i + 1) * 128])
            dst = cc[:, i * T + NP:(i + 1) * T]
            for t in range(31):
                src = Ei[:, VS + t * 16: VS + t * 16 + NS]
                w = V(("cw", l, t), i)
                if t == 0:
                    S.op("dve", lambda e, src=src, w=w, dst=dst, i=i, l=l: e.tensor_scalar(out=dst, in0=src, scalar1=w, scalar2=V(("cb", l), i), op0=ALU.mult, op1=ALU.add),
                         reads=[Er, "vecs"], writes=[("ccs", i), ("cc", i)])
                else:
                    S.op("dve", lambda e, src=src, w=w, dst=dst: e.scalar_tensor_tensor(out=dst, in0=src, scalar=w, in1=dst, op0=ALU.mult, op1=ALU.add),
                         reads=[Er, "vecs", ("ccs", i)], writes=[("ccs", i)])
            pending_conv.append(lambda i=i, Ei=Ei, Er=Er: cconv_pe(i, Ei, Er))
        for f in pending_conv:
            f()
        pending_conv.clear()
        norm_stats(lambda c: cc[:, c * T:(c + 1) * T], lambda c: [("cc", c), ("ccs", c)], 6, 1.0 / D_CC)
        for i in range(6):
            yt, ytr = tbuf[i % 2], ("t", i % 2)
            S.op("dve", lambda e, i=i, yt=yt, l=l: e.scalar_tensor_tensor(
                out=yt[:, 0:T], in0=cc[:, i * T:(i + 1) * T], scalar=V(("cg", l), i), in1=rstd[:], op0=ALU.mult, op1=ALU.mult),
                reads=[("cc", i), ("ccs", i), "rstd", "vecs"], writes=[ytr])
            S.op("act", lambda e, i=i, yt=yt: e.activation(out=mixbf(2 * i), in_=yt[:, 0:T], func=AF.Silu),
                 reads=[ytr], writes=[("cc", i)])
        out_proj_partial(l, lambda k: mixbf(2 * k), lambda k: [("cc", k)], 10, 6)

        FP_OFF, FS_OFF = 2, 2 + NP
        zero_halos()
        GSZ = [6, 6, 6, 6, 5, 5, 5, 5]
        GOFF = [sum(GSZ[:g]) for g in range(len(GSZ))]
        ngrp = len(GSZ)
        ectr = {"i": 0}

        def ffn_state_load(grp):
            f0 = GOFF[grp]
            nf = GSZ[grp]
            st, sr = stT[grp % 2], ("Es", grp % 2)
            for side in range(2):
                c0 = side * D_FF + f0 * 128
                src = stf_d[l, :, c0:c0 + nf * 128].rearrange("r (ch q c) -> q r ch c", q=4, c=32)
                for q in range(4):
                    S.op("sp", lambda e, st=st, src=src, q=q, side=side, nf=nf: e.dma_start(
                        out=st[32 * q:32 * q + 32, side * nf:(side + 1) * nf, :], in_=src[q]),
                        writes=[sr], dma=True)

        def ffn_up(grp):
            f0 = GOFF[grp]
            nf = GSZ[grp]
            ab = (grp % 2) * GF
            st, sr = stT[grp % 2], ("Es", grp % 2)
            if grp + 1 < ngrp:
                ffn_state_load(grp + 1)
            for fi in range(nf):
                f = f0 + fi
                for side in range(2):
                    col0 = side * D_FF + f * 128
                    uc = side * 44 + f
                    ch = side * nf + fi
                    Ei, Er = Ebuf[ectr["i"] % 3], ("E", ectr["i"] % 3)
                    ectr["i"] += 1
                    S.op("dve", lambda e, Ei=Ei, st=st, ch=ch: e.transpose(out=Ei[:, FS_OFF:FS_OFF + 32], in_=st[:, ch, :]),
                         reads=[sr], writes=[Er])
                    psn = proj_chunk(w_up_d, l, col0, lambda k: hT[:, k, :], H_RES)
                    ps = PS[psn]
                    S.op("act", lambda e, ps=ps, Ei=Ei: e.activation(out=Ei[:, FP_OFF:FP_OFF + NP], in_=ps[:, 0:NP], func=AF.Copy),
                         reads=[psn], writes=[Er])
                    S.op("act", lambda e, ps=ps, Ei=Ei: e.activation(out=Ei[:, FS_OFF + 32:FS_OFF + 160], in_=ps[:, NP:T], func=AF.Copy),
                         reads=[psn], writes=[Er])
                    if side == 1:
                        S.op("act", lambda e: e.activation(out=tbuf[0][:, 0:T], in_=tbuf[0][:, 0:T], func=AF.Silu),
                             reads=[("t", 0)], writes=[("t", 0)])
                    S.op("dve", lambda e, Ei=Ei, ch=ch: e.transpose(out=oT3[:, ch, :], in_=Ei[:, FS_OFF + 32 + 96:FS_OFF + 160]),
                         reads=[Er], writes=[("Es", 2)])
                    S.op("dve", lambda e, Ei=Ei, ch=ch: e.transpose(out=oP3[:, ch, :], in_=Ei[:, FP_OFF + NP - 32:FP_OFF + NP]),
                         reads=[Er], writes=STG_ALL)
                    acc, accr = tbuf[side], ("t", side)
                    conv_taps(acc, Ei, 0, FS_OFF, "fw", l, uc, 3, [Er], [accr])
                S.op("dve", lambda e, m=ab + fi: e.tensor_tensor(out=mixbf(m), in0=tbuf[0][:, 0:T], in1=tbuf[1][:, 0:T], op=ALU.mult),
                     reads=[("t", 0), ("t", 1)], writes=[ccres_bf(ab + fi)])
            for side in range(2):
                c0 = side * D_FF + f0 * 128
                dss = o_ffn_s[l, :, c0:c0 + nf * 128].rearrange("r (ch q c) -> q r ch c", q=4, c=32)
                dsp = o_ffn_p[l, :, c0:c0 + nf * 128].rearrange("r (ch q c) -> q r ch c", q=4, c=32)
                for q in range(4):
                    S.op("sp", lambda e, dss=dss, q=q, side=side, nf=nf: e.dma_start(
                        out=dss[q], in_=oT3[32 * q:32 * q + 32, side * nf:(side + 1) * nf, :]),
                        reads=[("Es", 2)], dma=True)
                    S.op("sp", lambda e, dsp=dsp, q=q, side=side, nf=nf: e.dma_start(
                        out=dsp[q], in_=oP3[32 * q + 30:32 * q + 32, side * nf:(side + 1) * nf, :]),
                        reads=STG_ALL, dma=True)

        def ffn_down(grp):
            f0 = GOFF[grp]
            nf = GSZ[grp]
            ab = (grp % 2) * GF
            act_res = sorted(set(ccres_bf(ab + k) for k in range(nf)))
            for dc in range(16):
                base, wres = wload(w_down_d[l, f0 * 128:(f0 + nf) * 128, dc * 128:(dc + 1) * 128].rearrange("(kc p) c -> p kc c", p=128), nf)
                psn = next_ps()
                pend = deferred[:]
                deferred.clear()
                mm_group(psn, lambda k, base=base: wr[:, base + k, :], lambda k, ab=ab: mixbf(ab + k), nf, reads=wres + act_res)
                for fdef in pend:
                    fdef()
                ps = PS[psn]
                S.op("dve", lambda e, ps=ps, dc=dc: e.tensor_tensor(out=xT[:, dc, :], in0=ps[:, 0:T], in1=xT[:, dc, :], op=ALU.add),
                     reads=[psn, ("x", dc)], writes=[("x", dc)])

        ffn_state_load(0)
        rmsnorm_to_h(("n2g", l))
        ffn_up(0)
        for grp in range(ngrp):
            if grp + 1 < ngrp:
                ffn_up(grp + 1)
            ffn_down(grp)
        flush_deferred()

        S.op("sp", lambda e, l=l: e.dma_start(out=o_pool_s[l, 0:7 * 16, :], in_=stp_d[l, 8 * 16:15 * 16, :]), dma=True)
        S.op("sp", lambda e, l=l: e.dma_start(out=o_cconv_s[l, 0:22 * 16, :], in_=stc_d[l, 8 * 16:30 * 16, :]), dma=True)

    norm_stats(lambda c: xT[:, c, :], lambda c: [("x", c)], 16, 1.0 / D)
    for c in range(16):
        S.op("dve", lambda e, c=c: e.scalar_tensor_tensor(
            out=xT[:, c, :], in0=xT[:, c, :], scalar=V(("fng",), c), in1=rstd[:], op0=ALU.mult, op1=ALU.mult),
            reads=[("x", c), "rstd", "vecs"], writes=[("x", c)])
    XRES = [("x", c) for c in range(16)]
    for ti, (src, r0, n, col0) in enumerate(tiles):
        dst = yp_d if src is xp_d else ys_d
        xi = xin[ti % 3]
        xr = xin_res[ti % 3]
        for half in range(2):
            psn = next_ps()
            ps = PS[psn]
            for j in range(8):
                kc = half * 8 + j
                S.op("pe", lambda e, ps=ps, j=j, kc=kc, n=n, col0=col0: e.transpose(
                    out=ps[0:n, j * 128:(j + 1) * 128], in_=xT[:, kc, col0:col0 + n], identity=identF[:]),
                    reads=[("x", kc), "identF"], writes=[psn])
            if half == 0:
                S.op("act", lambda e, ps=ps, xi=xi, n=n: e.activation(out=xi[0:n, 0:1024], in_=ps[0:n, 0:1024], func=AF.Copy),
                     reads=[psn], writes=xr)
            else:
                S.op("dve", lambda e, ps=ps, xi=xi, n=n: e.tensor_copy(out=xi[0:n, 1024:2048], in_=ps[0:n, 0:1024]),
                     reads=[psn], writes=xr)
        S.op("sp", lambda e, xi=xi, dst=dst, r0=r0, n=n: e.dma_start(out=dst[r0:r0 + n, :], in_=xi[0:n, :]),
             reads=xr, dma=True)

    fns = {"pe": block.tensor, "act": block.scalar, "dve": block.vector, "pool": block.gpsimd, "sp": block.sync}
    S.emit(fns, sems, dsems)
    es.close()
    return nc


def _jmajor(a):
    return np.ascontiguousarray(np.transpose(a, (1, 0, 2))).reshape(a.shape[1] * a.shape[0], a.shape[2])


def _unjmajor(a, R):
    return np.ascontiguousarray(np.transpose(a.reshape(R, 16, a.shape[-1]), (1, 0, 2)))


def kernel(**inp):
    inp = {k: np.asarray(v) for k, v in inp.items()}
    if "nc" not in _NC_CACHE:
        _NC_CACHE["nc"] = build_program()
    nc = _NC_CACHE["nc"]
    vecs = _pack_vecs(inp)
    ident = np.eye(128, dtype=np.float32)
    in_maps = []
    for c in range(NCORES):
        b, half = c // 2, c % 2
        t0 = 0 if half == 0 else SEQ - NP
        cnt = np.zeros((128, 4, 16), np.float32)
        for g, k in enumerate((2, 4, 8, 16)):
            for t in range(16):
                cnt[:, g, t] = 1.0 / min(t0 + t + 1, k)
        sl = slice(16 * c, 16 * c + 16)
        m = {
            "xp": np.ascontiguousarray(inp["x_prompt"][b, t0:t0 + NP]),
            "xs": _jmajor(inp["x_sample"][sl]),
            "st_pool": np.stack([_jmajor(inp["state_pool"][l, sl]) for l in range(DEPTH)]),
            "st_sconv": np.stack([_jmajor(inp["state_sconv"][l, sl]) for l in range(DEPTH)]),
            "st_cconv": np.stack([_jmajor(inp["state_cconv"][l, sl]) for l in range(DEPTH)]),
            "st_ffn": np.stack([_jmajor(inp["state_ffn"][l, sl]) for l in range(DEPTH)]),
            "vecs": vecs,
            "cnt": cnt.reshape(128, 64),
            "ident": ident,
            "w_in": inp["w_in"],
            "pool_w": inp["pool_w"],
            "w_out": inp["w_out"],
            "w_up": inp["w_up"],
            "w_down": inp["w_down"],
        }
        in_maps.append(m)
    res = run_bass_kernel_spmd(nc, in_maps, core_ids=list(range(NCORES)))
    R = res.results
    B = 4
    y_prompt = np.zeros((B, SEQ, D), np.float32)
    y_sample = np.zeros((128, 8, D), np.float32)
    new_pool_p = np.zeros((DEPTH, B, 15, D_POOL), np.float32)
    new_sconv_p = np.zeros((DEPTH, B, 2, D_SC), np.float32)
    new_cconv_p = np.zeros((DEPTH, B, 30, D_CC), np.float32)
    new_ffn_p = np.zeros((DEPTH, B, 2, 2 * D_FF), np.float32)
    new_pool_s = np.zeros((DEPTH, 128, 15, D_POOL), np.float32)
    new_sconv_s = np.zeros((DEPTH, 128, 2, D_SC), np.float32)
    new_cconv_s = np.zeros((DEPTH, 128, 30, D_CC), np.float32)
    new_ffn_s = np.zeros((DEPTH, 128, 2, 2 * D_FF), np.float32)
    for c in range(NCORES):
        b, half = c // 2, c % 2
        r = R[c]
        sl = slice(16 * c, 16 * c + 16)
        if half == 0:
            y_prompt[b, 0:NP] = r["yp"]
        else:
            y_prompt[b, NP:SEQ] = r["yp"][2 * NP - SEQ:]
            for l in range(DEPTH):
                new_pool_p[l, b] = r["o_pool_p"][l]
                new_sconv_p[l, b] = r["o_sconv_p"][l]
                new_cconv_p[l, b] = r["o_cconv_p"][l]
                new_ffn_p[l, b] = r["o_ffn_p"][l]
        y_sample[sl] = _unjmajor(r["ys"], 8)
        for l in range(DEPTH):
            new_pool_s[l, sl] = _unjmajor(r["o_pool_s"][l], 15)
            new_sconv_s[l, sl] = _unjmajor(r["o_sconv_s"][l], 2)
            new_cconv_s[l, sl] = _unjmajor(r["o_cconv_s"][l], 30)
            new_ffn_s[l, sl] = _unjmajor(r["o_ffn_s"][l], 2)
    return (y_prompt, y_sample, new_pool_p, new_sconv_p, new_cconv_p, new_ffn_p,
            new_pool_s, new_sconv_s, new_cconv_s, new_ffn_s)
```

```python
from contextlib import ExitStack
import numpy as np
import concourse.bass as bass
import concourse.mybir as mybir
from concourse.bass_utils import run_bass_kernel_spmd

F32 = mybir.dt.float32
BF16 = mybir.dt.bfloat16
AF = mybir.ActivationFunctionType
ALU = mybir.AluOpType

D = 2048
DEPTH = 2
SEQ = 2048
D_POOL, D_SC, D_CC = 512, 768, 768
D_FF = 5632
D_IN = 4352
EPS = 1e-6
NP = 1056
NS = 128
T = NP + NS
BLKS = [(0, 512), (512, 512), (1024, 160)]
NCORES = 8
WSLOTS = 5
GF = 6
ENGS = ("pe", "act", "dve", "pool", "sp")


class Op:
    __slots__ = ("eng", "fn", "deps", "idx", "milestone", "mnum", "is_dma",
                 "dma_sem", "dma_val", "pre_dma_wait")

    def __init__(self, eng, fn):
        self.eng = eng
        self.fn = fn
        self.deps = []
        self.idx = -1
        self.milestone = False
        self.mnum = 0
        self.is_dma = False
        self.dma_sem = None
        self.dma_val = 0
        self.pre_dma_wait = None


class Sched:
    def __init__(self, n_dma_sems=12):
        self.ops = {e: [] for e in ENGS}
        self.res_w = {}
        self.res_r = {}
        self.n_dma_sems = n_dma_sems
        self.dma_count = {e: 0 for e in ENGS}

    def op(self, eng, fn, reads=(), writes=(), dma=False):
        o = Op(eng, fn)
        best = {}
        dmadeps = {}

        def add(d):
            if d is None:
                return
            if d.is_dma:
                dmadeps[id(d)] = d
            else:
                b = best.get(d.eng)
                if b is None or d.idx > b.idx:
                    best[d.eng] = d

        for r in reads:
            add(self.res_w.get(r))
        for r in writes:
            add(self.res_w.get(r))
            rr = self.res_r.get(r)
            if rr:
                for rd in rr.values():
                    add(rd)
        o.deps = list(best.values()) + list(dmadeps.values())
        o.idx = len(self.ops[eng])
        o.is_dma = dma
        for r in reads:
            rr = self.res_r.setdefault(r, {})
            rr[("d", id(o)) if dma else eng] = o
        for r in writes:
            self.res_w[r] = o
            self.res_r[r] = {}
        self.ops[eng].append(o)
        if dma:
            n = self.dma_count[eng]
            self.dma_count[eng] = n + 1
            k = n % self.n_dma_sems
            o.dma_sem = (eng, k)
            o.dma_val = 16 * (n // self.n_dma_sems + 1)
            if n >= self.n_dma_sems:
                o.pre_dma_wait = ((eng, k), 16 * (n // self.n_dma_sems))
        return o

    def emit(self, block_fns, sems, dma_sems):
        plans = {}
        for e in ENGS:
            known = {f: -1 for f in ENGS}
            known_dma = {}
            plan = []
            for o in self.ops[e]:
                waits = []
                if o.pre_dma_wait is not None:
                    s, v = o.pre_dma_wait
                    if known_dma.get(s, 0) < v:
                        known_dma[s] = v
                        waits.append(("dma", s, v))
                for d in o.deps:
                    if d.is_dma:
                        if known_dma.get(d.dma_sem, 0) < d.dma_val:
                            known_dma[d.dma_sem] = d.dma_val
                            waits.append(("dma", d.dma_sem, d.dma_val))
                    else:
                        if d.eng == e and e == "pe":
                            continue
                        if known[d.eng] >= d.idx:
                            continue
                        known[d.eng] = d.idx
                        d.milestone = True
                        waits.append(("eng", d))
                plan.append(waits)
            plans[e] = plan
        for e in ENGS:
            m = 0
            for o in self.ops[e]:
                if o.milestone:
                    m += 1
                    o.mnum = m
        final_dma = {}
        for e in ENGS:
            for o in self.ops[e]:
                if o.is_dma:
                    final_dma[o.dma_sem] = max(final_dma.get(o.dma_sem, 0), o.dma_val)

        def make(e):
            def body(eng):
                for o, waits in zip(self.ops[e], plans[e]):
                    for w in waits:
                        if w[0] == "dma":
                            eng.wait_ge(dma_sems[w[1]], w[2])
                        else:
                            eng.wait_ge(sems[w[1].eng], w[1].mnum)
                    ins = o.fn(eng)
                    if o.is_dma:
                        ins.then_inc(dma_sems[o.dma_sem], 16)
                    elif o.milestone:
                        ins.then_inc(sems[e], 1)
                if e == "sp":
                    for s, v in sorted(final_dma.items()):
                        eng.wait_ge(dma_sems[s], v)
            return body

        for e in ENGS:
            block_fns[e](make(e))


def _vec_layout():
    off = {}
    n = 0

    def add(name, c):
        nonlocal n
        off[name] = n
        n += c

    for l in range(DEPTH):
        add(("n1g", l), 16)
        add(("n2g", l), 16)
        add(("psc", l), 4)
        for t in range(3):
            add(("sw", l, t), 6)
        for t in range(31):
            add(("cw", l, t), 6)
        add(("cb", l), 6)
        add(("cg", l), 6)
        for t in range(3):
            add(("fw", l, t), 88)
    add(("fng",), 16)
    return off, n


VOFF, NV = _vec_layout()


def _pack_vecs(inp):
    v = np.zeros((128, NV), np.float32)

    def put(name, arr):
        a = np.asarray(arr, np.float32).reshape(-1, 128).T
        v[:, VOFF[name]:VOFF[name] + a.shape[1]] = a

    for l in range(DEPTH):
        put(("n1g", l), inp["norm1_g"][l])
        put(("n2g", l), inp["norm2_g"][l])
        put(("psc", l), inp["pool_scale"][l])
        for t in range(3):
            put(("sw", l, t), inp["sconv_w"][l, t])
        for t in range(31):
            put(("cw", l, t), inp["cconv_w"][l, t])
        put(("cb", l), inp["cconv_b"][l])
        put(("cg", l), inp["cconv_norm_g"][l])
        for t in range(3):
            put(("fw", l, t), inp["ffn_conv_w"][l, t])
    put(("fng",), inp["final_norm_g"])
    return v


_NC_CACHE = {}


def build_program():
    nc = bass.Bass("TRN2", target_bir_lowering=False)

    def din(name, shape):
        return nc.dram_tensor(name, list(shape), F32, kind="ExternalInput").ap()

    def dout(name, shape):
        return nc.dram_tensor(name, list(shape), F32, kind="ExternalOutput").ap()

    xp_d = din("xp", [NP, D])
    xs_d = din("xs", [NS, D])
    stp_d = din("st_pool", [DEPTH, 240, D_POOL])
    sts_d = din("st_sconv", [DEPTH, 32, D_SC])
    stc_d = din("st_cconv", [DEPTH, 480, D_CC])
    stf_d = din("st_ffn", [DEPTH, 32, 2 * D_FF])
    vecs_d = din("vecs", [128, NV])
    cnt_d = din("cnt", [128, 4 * 16])
    ident_d = din("ident", [128, 128])
    w_in_d = din("w_in", [DEPTH, D, D_IN])
    pool_w_d = din("pool_w", [DEPTH, 4, 128, 128])
    w_out_d = din("w_out", [DEPTH, D, D])
    w_up_d = din("w_up", [DEPTH, D, 2 * D_FF])
    w_down_d = din("w_down", [DEPTH, D_FF, D])

    yp_d = dout("yp", [NP, D])
    ys_d = dout("ys", [NS, D])
    o_pool_p = dout("o_pool_p", [DEPTH, 15, D_POOL])
    o_sconv_p = dout("o_sconv_p", [DEPTH, 2, D_SC])
    o_cconv_p = dout("o_cconv_p", [DEPTH, 30, D_CC])
    o_ffn_p = dout("o_ffn_p", [DEPTH, 2, 2 * D_FF])
    o_pool_s = dout("o_pool_s", [DEPTH, 240, D_POOL])
    o_sconv_s = dout("o_sconv_s", [DEPTH, 32, D_SC])
    o_cconv_s = dout("o_cconv_s", [DEPTH, 480, D_CC])
    o_ffn_s = dout("o_ffn_s", [DEPTH, 32, 2 * D_FF])

    S = Sched()
    es = ExitStack()

    def sb(name, shape, dt):
        return es.enter_context(nc.sbuf_tensor("sb_" + name, shape, dt))

    xT = sb("xT", [128, 16, T], F32)
    hT = sb("hT", [128, 16, T], BF16)
    NHALF = 2 * WSLOTS
    wr = sb("wr", [128, NHALF * 8, 128], BF16)
    vecs = sb("vecs", [128, NV], F32)
    rstd = sb("rstd", [128, T], F32)
    sqb = [sb(f"sq{i}", [128, T], BF16) for i in range(2)]
    identF = sb("identF", [128, 128], F32)
    onesB = sb("onesB", [128, 128], BF16)
    epsT = sb("epsT", [128, 1], F32)
    identB = sb("identB", [128, 128], BF16)
    dg = sb("dg", [128, 4, 128], BF16)
    dg_rot = {"i": 0}
    cnt = sb("cnt", [128, 4, 16], F32)
    poolw = sb("poolw", [128, 4, 128], BF16)
    CC_W = 6 * T
    EW = 1704
    ARENA_W = CC_W + 3 * EW + 2 * T + 512 + 256
    arena = sb("arena", [128, ARENA_W], F32)
    cc = arena[:, 0:CC_W]
    cc_bf = cc.bitcast(BF16)
    Ebuf = [arena[:, CC_W + k * EW: CC_W + (k + 1) * EW] for k in range(3)]
    o_t = CC_W + 3 * EW
    tbuf = [arena[:, o_t + k * T: o_t + (k + 1) * T] for k in range(2)]
    o_s = o_t + 2 * T
    stg = arena[:, o_s:o_s + 512]
    tailstg = arena[:, o_s + 512:o_s + 768]

    pA = es.enter_context(nc.psum_tensor("pA", [128, 1536], F32))
    pB = es.enter_context(nc.psum_tensor("pB", [128, 1536], F32))
    pC = es.enter_context(nc.psum_tensor("pC", [128, 1024], F32))
    PS = {"pA": pA, "pB": pB}

    sems = {e: es.enter_context(nc.semaphore(f"s_{e}")) for e in ENGS}
    dsems = {}
    for e in ("sp", "pool", "act"):
        for k in range(S.n_dma_sems):
            dsems[(e, k)] = es.enter_context(nc.semaphore(f"d_{e}{k}"))
    block = es.enter_context(nc.Block())

    def V(name, c=0, n=1):
        o = VOFF[name] + c
        return vecs[:, o:o + n]

    def ccres_bf(m):
        return ("cc", m // 2)

    def mixbf(m):
        return cc_bf[:, m * T:(m + 1) * T]

    ps_state = {"i": 0}

    def next_ps():
        n = ("pA", "pB")[ps_state["i"] % 2]
        ps_state["i"] += 1
        return n

    evac_state = {"i": 0}

    wstate = {"i": 0}

    def wload(src_ap, nk):
        p = wstate["i"]
        if nk > 8:
            if p % 2 == 1:
                p += 1
            hs = [p % NHALF, (p + 1) % NHALF]
            wstate["i"] = p + 2
        else:
            hs = [p % NHALF]
            wstate["i"] = p + 1
        base = hs[0] * 8
        res = [("w", h) for h in hs]
        S.op("pool", lambda e: e.dma_start(out=wr[:, base:base + nk, :], in_=src_ap),
             writes=res, dma=True)
        return base, res

    def wcols(wd, l, rows0, nk, col0):
        return wd[l, rows0:rows0 + nk * 128, col0:col0 + 128].rearrange("(kc p) c -> p kc c", p=128)

    def mm_group(psn, lhs_fn, rhs_fn, nk, reads, kreads=None):
        ps = PS[psn]
        for k in range(nk):
            rk = reads if kreads is None else reads + kreads(k)
            for (c0, n) in BLKS:
                S.op("pe", lambda e, k=k, c0=c0, n=n: e.matmul(
                    ps[:, c0:c0 + n], lhsT=lhs_fn(k), rhs=rhs_fn(k)[:, c0:c0 + n],
                    start=(k == 0), stop=(k == nk - 1)),
                    reads=rk, writes=[psn])

    def proj_chunk(wd, l, col0, src_fn, src_res, nk=16, rows0=0):
        base, wres = wload(wcols(wd, l, rows0, nk, col0), nk)
        psn = next_ps()
        pend = deferred[:]
        deferred.clear()
        if callable(src_res):
            mm_group(psn, lambda k: wr[:, base + k, :], src_fn, nk, reads=wres, kreads=src_res)
        else:
            mm_group(psn, lambda k: wr[:, base + k, :], src_fn, nk, reads=wres + src_res)
        for f in pend:
            f()
        return psn

    deferred = []

    def flush_deferred():
        pend = deferred[:]
        deferred.clear()
        for f in pend:
            f()

    def norm_stats(src_fn, src_res, nch, inv_n, bias_fn=None):
        psn = next_ps()
        ps = PS[psn]
        for c in range(nch):
            sq = sqb[c % 2]
            sqr = ("sq", c % 2)
            if bias_fn is None and c % 2 == 1:
                S.op("dve", lambda e, c=c, sq=sq: e.tensor_tensor(out=sq[:], in0=src_fn(c), in1=src_fn(c), op=ALU.mult),
                     reads=src_res(c), writes=[sqr])
            elif bias_fn is None:
                S.op("act", lambda e, c=c, sq=sq: e.activation(out=sq[:], in_=src_fn(c), func=AF.Square),
                     reads=src_res(c), writes=[sqr])
            else:
                S.op("act", lambda e, c=c, sq=sq: e.activation(out=sq[:], in_=src_fn(c), func=AF.Square, bias=bias_fn(c)),
                     reads=src_res(c), writes=[sqr])
            for (c0, n) in BLKS:
                S.op("pe", lambda e, c=c, c0=c0, n=n, sq=sq: e.matmul(
                    ps[:, c0:c0 + n], lhsT=onesB[:], rhs=sq[:, c0:c0 + n],
                    start=(c == 0), stop=(c == nch - 1)),
                    reads=[sqr, "onesB"], writes=[psn])
        S.op("act", lambda e: e.activation(out=rstd[:], in_=ps[:, 0:T], func=AF.Sqrt, bias=epsT[:], scale=inv_n),
             reads=[psn, "epsT"], writes=["rstd"])
        S.op("dve", lambda e: e.reciprocal(out=rstd[:], in_=rstd[:]),
             reads=["rstd"], writes=["rstd"])

    def rmsnorm_to_h(gname):
        norm_stats(lambda c: xT[:, c, :], lambda c: [("x", c)], 16, 1.0 / D)
        for c in range(16):
            S.op("dve", lambda e, c=c: e.scalar_tensor_tensor(
                out=hT[:, c, :], in0=xT[:, c, :], scalar=V(gname, c), in1=rstd[:],
                op0=ALU.mult, op1=ALU.mult),
                reads=[("x", c), "rstd", "vecs"], writes=[("h", c)])

    H_RES = lambda k: [("h", k)]

    cstate = {"i": 0}

    def next_c():
        k = cstate["i"] % 2
        cstate["i"] += 1
        return k

    def load_state_T(src_rows_fn, nrows_total, dst_ap_fn):
        ntile = (nrows_total + 119) // 120
        rows = []
        r0 = 0
        for i in range(ntile):
            n = min(120, nrows_total - r0)
            rows.append((r0, n))
            r0 += n
        ck = next_c()
        cres = ("pC", ck)
        if ntile == 1:
            tl = [stg_rot["i"] % 4]
            stg_rot["i"] += 1
        else:
            tl = list(range(ntile))
        for i, (r0, n) in enumerate(rows):
            ti = tl[i]
            S.op("sp", lambda e, ti=ti, r0=r0, n=n: e.dma_start(out=stg[0:n, ti * 128:(ti + 1) * 128], in_=src_rows_fn(r0, n)),
                 writes=[("stg", ti)], dma=True)
        for i, (r0, n) in enumerate(rows):
            ti = tl[i]
            S.op("pe", lambda e, ti=ti, r0=r0, n=n: e.transpose(
                out=pC[:, ck * 512 + r0: ck * 512 + r0 + n], in_=stg[0:n, ti * 128:(ti + 1) * 128],
                identity=identF[0:n, 0:n]),
                reads=[("stg", ti), "identF"], writes=[cres])
        return ck, cres

    stg_rot = {"i": 0}

    tstate = {"i": 0}

    def store_tail(src_ap, ncols, src_res, dst_ap):
        deferred.append(lambda: _store_tail(src_ap, ncols, src_res, dst_ap))

    def _store_tail(src_ap, ncols, src_res, dst_ap):
        ck = next_c()
        cres = ("pC", ck)
        S.op("pe", lambda e: e.transpose(out=pC[0:ncols, ck * 512: ck * 512 + 128], in_=src_ap, identity=identF[:]),
             reads=list(src_res) + ["identF"], writes=[cres])
        k = tstate["i"] % 2
        tstate["i"] += 1
        ts = tailstg[:, k * 128:(k + 1) * 128]
        S.op("act", lambda e: e.activation(out=ts[0:ncols, :], in_=pC[0:ncols, ck * 512: ck * 512 + 128], func=AF.Copy),
             reads=[cres], writes=[("tail", k)])
        S.op("act", lambda e: e.dma_start(out=dst_ap, in_=ts[0:ncols, :]), reads=[("tail", k)], dma=True)

    S.op("sp", lambda e: e.dma_start(out=vecs[:], in_=vecs_d), writes=["vecs"], dma=True)
    S.op("sp", lambda e: e.dma_start(out=identF[:], in_=ident_d), writes=["identF"], dma=True)
    S.op("sp", lambda e: e.dma_start(out=cnt[:], in_=cnt_d.rearrange("p (a b) -> p a b", a=4)), writes=["cnt"], dma=True)
    S.op("dve", lambda e: e.memset(onesB[:], 1.0), writes=["onesB"])
    S.op("dve", lambda e: e.memset(epsT[:], EPS), writes=["epsT"])
    S.op("dve", lambda e: e.tensor_copy(out=identB[:], in_=identF[:]), reads=["identF"], writes=["identB"])
    for k in range(3):
        S.op("dve", lambda e, k=k: e.memset(Ebuf[k], 0.0), writes=[("E", k)])

    xin = [arena[:, 0:2048], arena[:, 2368:2368 + 2048], arena[:, 4736:4736 + 2048]]
    xin_res = [[("cc", 0), ("cc", 1)], [("cc", 2), ("cc", 3)], [("cc", 4), ("cc", 5)]]
    tiles = [(xp_d, i * 128, 128, i * 128) for i in range(8)] + [(xp_d, 1024, 32, 1024), (xs_d, 0, 128, NP)]
    for ti, (src, r0, n, col0) in enumerate(tiles):
        xi = xin[ti % 3]
        xr = xin_res[ti % 3]
        S.op("sp", lambda e, xi=xi, src=src, r0=r0, n=n: e.dma_start(out=xi[0:n, :], in_=src[r0:r0 + n, :]),
             writes=xr, dma=True)
        for half in range(2):
            psn = next_ps()
            ps = PS[psn]
            for j in range(8):
                kc = half * 8 + j
                S.op("pe", lambda e, ps=ps, j=j, kc=kc, xi=xi, n=n: e.transpose(
                    out=ps[:, j * 128: j * 128 + n], in_=xi[0:n, kc * 128:(kc + 1) * 128], identity=identF[0:n, 0:n]),
                    reads=xr + ["identF"], writes=[psn])
            src_v = ps[:, 0:1024].rearrange("p (a b) -> p a b", a=8)[:, :, 0:n]
            dst_v = xT[:, half * 8:(half + 1) * 8, col0:col0 + n]
            xres = [("x", c) for c in range(half * 8, half * 8 + 8)]
            if half == 0:
                S.op("act", lambda e, dst_v=dst_v, src_v=src_v: e.activation(out=dst_v, in_=src_v, func=AF.Copy),
                     reads=[psn], writes=xres)
            else:
                S.op("dve", lambda e, dst_v=dst_v, src_v=src_v: e.tensor_copy(out=dst_v, in_=src_v),
                     reads=[psn], writes=xres)

    def conv_taps(dst, ext, p_off, s_off, wname, l, ci, ntap, ext_res, dst_res, bias=None):
        for (d0, dn, e0, sh) in ((0, NP, p_off, 1), (NP, NS, s_off, 16)):
            for i in range(ntap):
                src = ext[:, e0 + i * sh: e0 + i * sh + dn]
                w = V((wname, l, i), ci)
                if i == 0:
                    if bias is None:
                        S.op("dve", lambda e, src=src, w=w, d0=d0, dn=dn: e.tensor_scalar(
                            out=dst[:, d0:d0 + dn], in0=src, scalar1=w, scalar2=None, op0=ALU.mult),
                            reads=ext_res + ["vecs"], writes=dst_res)
                    else:
                        S.op("dve", lambda e, src=src, w=w, d0=d0, dn=dn: e.tensor_scalar(
                            out=dst[:, d0:d0 + dn], in0=src, scalar1=w, scalar2=bias, op0=ALU.mult, op1=ALU.add),
                            reads=ext_res + ["vecs"], writes=dst_res)
                else:
                    S.op("dve", lambda e, src=src, w=w, d0=d0, dn=dn: e.scalar_tensor_tensor(
                        out=dst[:, d0:d0 + dn], in0=src, scalar=w, in1=dst[:, d0:d0 + dn],
                        op0=ALU.mult, op1=ALU.add),
                        reads=ext_res + ["vecs"] + dst_res, writes=dst_res)

    def out_proj_partial(l, mix_fn, mix_res, kc0, nk):
        for dc in range(16):
            psn = proj_chunk(w_out_d, l, dc * 128, mix_fn, mix_res, nk=nk, rows0=kc0 * 128)
            ps = PS[psn]
            S.op("dve", lambda e, ps=ps, dc=dc: e.tensor_tensor(out=xT[:, dc, :], in0=ps[:, 0:T], in1=xT[:, dc, :], op=ALU.add),
                 reads=[psn, ("x", dc)], writes=[("x", dc)])

    def zero_halos():
        for k in range(3):
            S.op("dve", lambda e, k=k: e.memset(Ebuf[k][:, 0:32], 0.0), writes=[("E", k), ("Es", k)])

    ES0 = 1218
    stT = [Ebuf[0][:, ES0:ES0 + 384].rearrange("p (a b) -> p a b", b=32), Ebuf[1][:, ES0:ES0 + 384].rearrange("p (a b) -> p a b", b=32)]
    oT3 = Ebuf[2][:, ES0:ES0 + 384].rearrange("p (a b) -> p a b", b=32)
    oP3 = stg[:, 0:384].rearrange("p (a b) -> p a b", b=32)
    STG_ALL = [("stg", i) for i in range(4)]

    for l in range(DEPTH):
        rmsnorm_to_h(("n1g", l))
        S.op("pool", lambda e, l=l: e.dma_start(out=poolw[:], in_=pool_w_d[l].rearrange("g c d -> c g d")),
             writes=["poolw"], dma=True)

        P_OFF, PS_OFF = 15, 15 + NP
        zero_halos()
        for g in range(4):
            kwin = (2, 4, 8, 16)[g]
            E0, E1, E2 = Ebuf[0], Ebuf[1], Ebuf[2]
            psn = proj_chunk(w_in_d, l, g * 128, lambda k: hT[:, k, :], H_RES)
            ps = PS[psn]
            ck, cres = load_state_T(lambda r0, n, g=g, l=l: stp_d[l, r0:r0 + n, g * 128:(g + 1) * 128], 240, None)
            S.op("act", lambda e, ck=ck: e.activation(out=E0[:, PS_OFF:PS_OFF + 240], in_=pC[:, ck * 512: ck * 512 + 240], func=AF.Copy),
                 reads=[cres], writes=[("E", 0)])
            S.op("act", lambda e, ps=ps: e.activation(out=E0[:, P_OFF:P_OFF + NP], in_=ps[:, 0:NP], func=AF.Copy),
                 reads=[psn], writes=[("E", 0)])
            S.op("act", lambda e, ps=ps: e.activation(out=E0[:, PS_OFF + 240:PS_OFF + 368], in_=ps[:, NP:T], func=AF.Copy),
                 reads=[psn], writes=[("E", 0)])
            store_tail(E0[:, P_OFF + NP - 15:P_OFF + NP], 15, [("E", 0)], o_pool_p[l, :, g * 128:(g + 1) * 128])
            store_tail(E0[:, PS_OFF + 240:PS_OFF + 368], 128, [("E", 0)], o_pool_s[l, 7 * 16:15 * 16, g * 128:(g + 1) * 128])
            cur, cur_r = E0, ("E", 0)
            others = [(E1, ("E", 1)), (E2, ("E", 2))]
            sh = 1
            step = 0
            while sh < kwin:
                nxt, nxt_r = others[step % 2]
                S.op("dve", lambda e, cur=cur, nxt=nxt, sh=sh: e.tensor_tensor(
                    out=nxt[:, sh:P_OFF + NP], in0=cur[:, sh:P_OFF + NP], in1=cur[:, 0:P_OFF + NP - sh], op=ALU.add),
                    reads=[cur_r], writes=[nxt_r])
                S.op("dve", lambda e, cur=cur, nxt=nxt, sh=sh: e.tensor_tensor(
                    out=nxt[:, PS_OFF + sh * 16:PS_OFF + 368], in0=cur[:, PS_OFF + sh * 16:PS_OFF + 368],
                    in1=cur[:, PS_OFF:PS_OFF + 368 - sh * 16], op=ALU.add),
                    reads=[cur_r], writes=[nxt_r])
                cur, cur_r = nxt, nxt_r
                sh *= 2
                step += 1
            dbf = tbuf[0].bitcast(BF16)[:, (g % 2) * T:(g % 2 + 1) * T]
            tdr = ("td", g % 2)
            inv = 1.0 / kwin
            S.op("dve", lambda e, cur=cur, inv=inv, dbf=dbf: e.scalar_tensor_tensor(
                out=dbf[:, 0:NP], in0=cur[:, P_OFF:P_OFF + NP], scalar=inv, in1=E0[:, P_OFF:P_OFF + NP],
                op0=ALU.mult, op1=ALU.subtract),
                reads=[cur_r, ("E", 0)], writes=[tdr])
            S.op("dve", lambda e, cur=cur, inv=inv, dbf=dbf: e.scalar_tensor_tensor(
                out=dbf[:, NP:T], in0=cur[:, PS_OFF + 240:PS_OFF + 368], scalar=inv, in1=E0[:, PS_OFF + 240:PS_OFF + 368],
                op0=ALU.mult, op1=ALU.subtract),
                reads=[cur_r, ("E", 0)], writes=[tdr])
            f16 = tbuf[1][:, 0:16]
            S.op("dve", lambda e, cur=cur, g=g: e.tensor_tensor(out=f16, in0=cur[:, P_OFF:P_OFF + 16], in1=cnt[:, g, :], op=ALU.mult),
                 reads=[cur_r, "cnt"], writes=[("t", 1)])
            S.op("dve", lambda e, dbf=dbf: e.tensor_tensor(out=dbf[:, 0:16], in0=f16, in1=E0[:, P_OFF:P_OFF + 16], op=ALU.subtract),
                 reads=[("t", 1), ("E", 0), tdr], writes=[tdr])

            def pool_mm(g=g, l=l, dbf=dbf, tdr=tdr):
                psn2 = next_ps()
                ps2 = PS[psn2]
                for (c0, n) in BLKS:
                    S.op("pe", lambda e, ps2=ps2, c0=c0, n=n, g=g, dbf=dbf: e.matmul(ps2[:, c0:c0 + n], lhsT=poolw[:, g, :], rhs=dbf[:, c0:c0 + n], start=True, stop=True),
                         reads=["poolw", tdr], writes=[psn2])
                S.op("act", lambda e, ps2=ps2, g=g, l=l: e.activation(out=mixbf(g), in_=ps2[:, 0:T], func=AF.Copy, scale=V(("psc", l), g)),
                     reads=[psn2, "vecs"], writes=[ccres_bf(g)])
            deferred.append(pool_mm)
        flush_deferred()
        out_proj_partial(l, lambda k: mixbf(k), lambda k: [ccres_bf(k)], 0, 4)
        zero_halos()

        o1 = D_POOL
        SP_OFF, SS_OFF = 2, 2 + NP
        for i in range(6):
            Ei, Er = Ebuf[i % 2], ("E", i % 2)
            gc, gcr = tbuf[0], ("t", 0)
            cbuf, cbr = tbuf[1], ("t", 1)
            psn = proj_chunk(w_in_d, l, o1 + D_SC + i * 128, lambda k: hT[:, k, :], H_RES)
            ps = PS[psn]
            ck, cres = load_state_T(lambda r0, n, i=i, l=l: sts_d[l, r0:r0 + n, i * 128:(i + 1) * 128], 32, None)
            S.op("act", lambda e, ck=ck, Ei=Ei: e.activation(out=Ei[:, SS_OFF:SS_OFF + 32], in_=pC[:, ck * 512: ck * 512 + 32], func=AF.Copy),
                 reads=[cres], writes=[Er])
            S.op("act", lambda e, ps=ps: e.activation(out=gc[:, 0:T], in_=ps[:, 0:T], func=AF.Copy),
                 reads=[psn], writes=[gcr])
            psn = proj_chunk(w_in_d, l, o1 + 2 * D_SC + i * 128, lambda k: hT[:, k, :], H_RES)
            ps = PS[psn]
            S.op("dve", lambda e, ps=ps, Ei=Ei: e.tensor_tensor(out=Ei[:, SP_OFF:SP_OFF + NP], in0=ps[:, 0:NP], in1=gc[:, 0:NP], op=ALU.mult),
                 reads=[psn, gcr], writes=[Er])
            S.op("dve", lambda e, ps=ps, Ei=Ei: e.tensor_tensor(out=Ei[:, SS_OFF + 32:SS_OFF + 160], in0=ps[:, NP:T], in1=gc[:, NP:T], op=ALU.mult),
                 reads=[psn, gcr], writes=[Er])
            store_tail(Ei[:, SP_OFF + NP - 2:SP_OFF + NP], 2, [Er], o_sconv_p[l, :, i * 128:(i + 1) * 128])
            store_tail(Ei[:, SS_OFF + 32 + 96:SS_OFF + 160], 32, [Er], o_sconv_s[l, :, i * 128:(i + 1) * 128])
            conv_taps(cbuf, Ei, 0, SS_OFF, "sw", l, i, 3, [Er], [cbr])
            psn = proj_chunk(w_in_d, l, o1 + i * 128, lambda k: hT[:, k, :], H_RES)
            ps = PS[psn]
            S.op("dve", lambda e, ps=ps, i=i: e.tensor_tensor(out=mixbf(i), in0=ps[:, 0:T], in1=cbuf[:, 0:T], op=ALU.mult),
                 reads=[psn, cbr], writes=[ccres_bf(i)])
        out_proj_partial(l, lambda k: mixbf(k), lambda k: [ccres_bf(k)], 4, 6)

        o4 = D_POOL + 3 * D_SC
        CP_OFF, CS_OFF = 30, 30 + NP
        zero_halos()
        VT_P, VS = 1000, 1040
        pending_conv = []

        def cconv_pe(i, Ei, Er, l=l):
            Ebf = Ei.bitcast(BF16)
            psn = next_ps()
            ps = PS[psn]
            for t in range(31):
                r = dg_rot["i"] % 4
                dg_rot["i"] += 1
                S.op("dve", lambda e, r=r, t=t: e.tensor_scalar(out=dg[:, r, :], in0=identB[:], scalar1=V(("cw", l, t), i), scalar2=None, op0=ALU.mult),
                     reads=["identB", "vecs"], writes=[("dg", r)])
                for (c0, n, src0) in ((0, 512, t), (512, 512, t + 512), (1024, 32, t + 1024)):
                    S.op("pe", lambda e, r=r, t=t, c0=c0, n=n, src0=src0: e.matmul(
                        ps[:, c0:c0 + n], lhsT=dg[:, r, :], rhs=Ebf[:, src0:src0 + n], start=(t == 0), stop=(t == 30)),
                        reads=[("dg", r), Er], writes=[psn])
            S.op("act", lambda e: e.activation(out=cc[:, i * T:i * T + NP], in_=ps[:, 0:NP], func=AF.Identity, bias=V(("cb", l), i)),
                 reads=[psn, "vecs"], writes=[("cc", i)])
            dst = cc[:, i * T + NP:(i + 1) * T]
            for t in range(31):
                src = Ei[:, VS + t * 16: VS + t * 16 + NS]
                w = V(("cw", l, t), i)
                if t == 0:
                    S.op("dve", lambda e, src=src, w=w: e.tensor_scalar(out=dst, in0=src, scalar1=w, scalar2=V(("cb", l), i), op0=ALU.mult, op1=ALU.add),
                         reads=[Er, "vecs"], writes=[("ccs", i)])
                else:
                    S.op("dve", lambda e, src=src, w=w: e.scalar_tensor_tensor(out=dst, in0=src, scalar=w, in1=dst, op0=ALU.mult, op1=ALU.add),
                         reads=[Er, "vecs", ("ccs", i)], writes=[("ccs", i)])

        for i in range(6):
            Ei, Er = Ebuf[i % 2], ("E", i % 2)
            Ebf = Ei.bitcast(BF16)
            sg, sgr = tbuf[0], ("t", 0)
            psn = proj_chunk(w_in_d, l, o4 + D_CC + i * 128, lambda k: hT[:, k, :], H_RES)
            ps = PS[psn]
            ck, cres = load_state_T(lambda r0, n, i=i, l=l: stc_d[l, r0:r0 + n, i * 128:(i + 1) * 128], 480, None)
            S.op("act", lambda e, ck=ck, Ei=Ei: e.activation(out=Ei[:, VS:VS + 480], in_=pC[:, ck * 512: ck * 512 + 480], func=AF.Copy),
                 reads=[cres], writes=[Er])
            for f in pending_conv:
                f()
            pending_conv.clear()
            S.op("act", lambda e, ps=ps: e.activation(out=sg[:, 0:T], in_=ps[:, 0:T], func=AF.Sigmoid),
                 reads=[psn], writes=[sgr])
            psn = proj_chunk(w_in_d, l, o4 + i * 128, lambda k: hT[:, k, :], H_RES)
            ps = PS[psn]
            S.op("dve", lambda e, ps=ps, Ebf=Ebf: e.tensor_tensor(out=Ebf[:, CP_OFF:CP_OFF + NP], in0=ps[:, 0:NP], in1=sg[:, 0:NP], op=ALU.mult),
                 reads=[psn, sgr], writes=[Er])
            S.op("dve", lambda e, ps=ps, Ei=Ei: e.tensor_tensor(out=Ei[:, VS + 480:VS + 608], in0=ps[:, NP:T], in1=sg[:, NP:T], op=ALU.mult),
                 reads=[psn, sgr], writes=[Er])
            S.op("dve", lambda e, ps=ps, Ei=Ei: e.tensor_tensor(out=Ei[:, VT_P:VT_P + 30], in0=ps[:, NP - 30:NP], in1=sg[:, NP - 30:NP], op=ALU.mult),
                 reads=[psn, sgr], writes=[Er])
            store_tail(Ei[:, VT_P:VT_P + 30], 30, [Er], o_cconv_p[l, :, i * 128:(i + 1) * 128])
            store_tail(Ei[:, VS + 480:VS + 608], 128, [Er], o_cconv_s[l, 22 * 16:30 * 16, i * 128:(i + 1) * 128])
            pending_conv.append(lambda i=i, Ei=Ei, Er=Er: cconv_pe(i, Ei, Er))
        for f in pending_conv:
            f()
        pending_conv.clear()
        norm_stats(lambda c: cc[:, c * T:(c + 1) * T], lambda c: [("cc", c), ("ccs", c)], 6, 1.0 / D_CC)
        for i in range(6):
            yt, ytr = tbuf[i % 2], ("t", i % 2)
            S.op("dve", lambda e, i=i, yt=yt, l=l: e.scalar_tensor_tensor(
                out=yt[:, 0:T], in0=cc[:, i * T:(i + 1) * T], scalar=V(("cg", l), i), in1=rstd[:], op0=ALU.mult, op1=ALU.mult),
                reads=[("cc", i), ("ccs", i), "rstd", "vecs"], writes=[ytr])
            S.op("act", lambda e, i=i, yt=yt: e.activation(out=mixbf(2 * i), in_=yt[:, 0:T], func=AF.Silu),
                 reads=[ytr], writes=[("cc", i)])
        out_proj_partial(l, lambda k: mixbf(2 * k), lambda k: [("cc", k)], 10, 6)

        FP_OFF, FS_OFF = 2, 2 + NP
        zero_halos()
        GSZ = [6, 6, 6, 6, 5, 5, 5, 5]
        GOFF = [sum(GSZ[:g]) for g in range(len(GSZ))]
        ngrp = len(GSZ)
        ectr = {"i": 0}

        def ffn_state_load(grp):
            f0 = GOFF[grp]
            nf = GSZ[grp]
            st, sr = stT[grp % 2], ("Es", grp % 2)
            for side in range(2):
                c0 = side * D_FF + f0 * 128
                src = stf_d[l, :, c0:c0 + nf * 128].rearrange("r (ch q c) -> q r ch c", q=4, c=32)
                for q in range(4):
                    S.op("sp", lambda e, st=st, src=src, q=q, side=side, nf=nf: e.dma_start(
                        out=st[32 * q:32 * q + 32, side * nf:(side + 1) * nf, :], in_=src[q]),
                        writes=[sr], dma=True)

        def ffn_up(grp):
            f0 = GOFF[grp]
            nf = GSZ[grp]
            ab = (grp % 2) * GF
            st, sr = stT[grp % 2], ("Es", grp % 2)
            if grp + 1 < ngrp:
                ffn_state_load(grp + 1)
            for fi in range(nf):
                f = f0 + fi
                for side in range(2):
                    col0 = side * D_FF + f * 128
                    uc = side * 44 + f
                    ch = side * nf + fi
                    Ei, Er = Ebuf[ectr["i"] % 3], ("E", ectr["i"] % 3)
                    ectr["i"] += 1
                    S.op("dve", lambda e, Ei=Ei, st=st, ch=ch: e.transpose(out=Ei[:, FS_OFF:FS_OFF + 32], in_=st[:, ch, :]),
                         reads=[sr], writes=[Er])
                    psn = proj_chunk(w_up_d, l, col0, lambda k: hT[:, k, :], H_RES)
                    ps = PS[psn]
                    S.op("act", lambda e, ps=ps, Ei=Ei: e.activation(out=Ei[:, FP_OFF:FP_OFF + NP], in_=ps[:, 0:NP], func=AF.Copy),
                         reads=[psn], writes=[Er])
                    S.op("act", lambda e, ps=ps, Ei=Ei: e.activation(out=Ei[:, FS_OFF + 32:FS_OFF + 160], in_=ps[:, NP:T], func=AF.Copy),
                         reads=[psn], writes=[Er])
                    if side == 1:
                        S.op("act", lambda e: e.activation(out=tbuf[0][:, 0:T], in_=tbuf[0][:, 0:T], func=AF.Silu),
                             reads=[("t", 0)], writes=[("t", 0)])
                    S.op("dve", lambda e, Ei=Ei, ch=ch: e.transpose(out=oT3[:, ch, :], in_=Ei[:, FS_OFF + 32 + 96:FS_OFF + 160]),
                         reads=[Er], writes=[("Es", 2)])
                    S.op("dve", lambda e, Ei=Ei, ch=ch: e.transpose(out=oP3[:, ch, :], in_=Ei[:, FP_OFF + NP - 32:FP_OFF + NP]),
                         reads=[Er], writes=STG_ALL)
                    acc, accr = tbuf[side], ("t", side)
                    conv_taps(acc, Ei, 0, FS_OFF, "fw", l, uc, 3, [Er], [accr])
                S.op("dve", lambda e, m=ab + fi: e.tensor_tensor(out=mixbf(m), in0=tbuf[0][:, 0:T], in1=tbuf[1][:, 0:T], op=ALU.mult),
                     reads=[("t", 0), ("t", 1)], writes=[ccres_bf(ab + fi)])
            for side in range(2):
                c0 = side * D_FF + f0 * 128
                dss = o_ffn_s[l, :, c0:c0 + nf * 128].rearrange("r (ch q c) -> q r ch c", q=4, c=32)
                dsp = o_ffn_p[l, :, c0:c0 + nf * 128].rearrange("r (ch q c) -> q r ch c", q=4, c=32)
                for q in range(4):
                    S.op("sp", lambda e, dss=dss, q=q, side=side, nf=nf: e.dma_start(
                        out=dss[q], in_=oT3[32 * q:32 * q + 32, side * nf:(side + 1) * nf, :]),
                        reads=[("Es", 2)], dma=True)
                    S.op("sp", lambda e, dsp=dsp, q=q, side=side, nf=nf: e.dma_start(
                        out=dsp[q], in_=oP3[32 * q + 30:32 * q + 32, side * nf:(side + 1) * nf, :]),
                        reads=STG_ALL, dma=True)

        def ffn_down(grp):
            f0 = GOFF[grp]
            nf = GSZ[grp]
            ab = (grp % 2) * GF
            act_res = sorted(set(ccres_bf(ab + k) for k in range(nf)))
            for dc in range(16):
                base, wres = wload(w_down_d[l, f0 * 128:(f0 + nf) * 128, dc * 128:(dc + 1) * 128].rearrange("(kc p) c -> p kc c", p=128), nf)
                psn = next_ps()
                pend = deferred[:]
                deferred.clear()
                mm_group(psn, lambda k, base=base: wr[:, base + k, :], lambda k, ab=ab: mixbf(ab + k), nf, reads=wres + act_res)
                for fdef in pend:
                    fdef()
                ps = PS[psn]
                S.op("dve", lambda e, ps=ps, dc=dc: e.tensor_tensor(out=xT[:, dc, :], in0=ps[:, 0:T], in1=xT[:, dc, :], op=ALU.add),
                     reads=[psn, ("x", dc)], writes=[("x", dc)])

        ffn_state_load(0)
        rmsnorm_to_h(("n2g", l))
        ffn_up(0)
        for grp in range(ngrp):
            if grp + 1 < ngrp:
                ffn_up(grp + 1)
            ffn_down(grp)
        flush_deferred()

        S.op("sp", lambda e, l=l: e.dma_start(out=o_pool_s[l, 0:7 * 16, :], in_=stp_d[l, 8 * 16:15 * 16, :]), dma=True)
        S.op("sp", lambda e, l=l: e.dma_start(out=o_cconv_s[l, 0:22 * 16, :], in_=stc_d[l, 8 * 16:30 * 16, :]), dma=True)

    norm_stats(lambda c: xT[:, c, :], lambda c: [("x", c)], 16, 1.0 / D)
    for c in range(16):
        S.op("dve", lambda e, c=c: e.scalar_tensor_tensor(
            out=xT[:, c, :], in0=xT[:, c, :], scalar=V(("fng",), c), in1=rstd[:], op0=ALU.mult, op1=ALU.mult),
            reads=[("x", c), "rstd", "vecs"], writes=[("x", c)])
    XRES = [("x", c) for c in range(16)]
    for ti, (src, r0, n, col0) in enumerate(tiles):
        dst = yp_d if src is xp_d else ys_d
        xi = xin[ti % 3]
        xr = xin_res[ti % 3]
        for half in range(2):
            psn = next_ps()
            ps = PS[psn]
            for j in range(8):
                kc = half * 8 + j
                S.op("pe", lambda e, ps=ps, j=j, kc=kc, n=n, col0=col0: e.transpose(
                    out=ps[0:n, j * 128:(j + 1) * 128], in_=xT[:, kc, col0:col0 + n], identity=identF[:]),
                    reads=[("x", kc), "identF"], writes=[psn])
            if half == 0:
                S.op("act", lambda e, ps=ps, xi=xi, n=n: e.activation(out=xi[0:n, 0:1024], in_=ps[0:n, 0:1024], func=AF.Copy),
                     reads=[psn], writes=xr)
            else:
                S.op("dve", lambda e, ps=ps, xi=xi, n=n: e.tensor_copy(out=xi[0:n, 1024:2048], in_=ps[0:n, 0:1024]),
                     reads=[psn], writes=xr)
        S.op("sp", lambda e, xi=xi, dst=dst, r0=r0, n=n: e.dma_start(out=dst[r0:r0 + n, :], in_=xi[0:n, :]),
             reads=xr, dma=True)

    fns = {"pe": block.tensor, "act": block.scalar, "dve": block.vector, "pool": block.gpsimd, "sp": block.sync}
    S.emit(fns, sems, dsems)
    es.close()
    return nc


def _jmajor(a):
    return np.ascontiguousarray(np.transpose(a, (1, 0, 2))).reshape(a.shape[1] * a.shape[0], a.shape[2])


def _unjmajor(a, R):
    return np.ascontiguousarray(np.transpose(a.reshape(R, 16, a.shape[-1]), (1, 0, 2)))


def kernel(**inp):
    inp = {k: np.asarray(v) for k, v in inp.items()}
    if "nc" not in _NC_CACHE:
        _NC_CACHE["nc"] = build_program()
    nc = _NC_CACHE["nc"]
    vecs = _pack_vecs(inp)
    ident = np.eye(128, dtype=np.float32)
    in_maps = []
    for c in range(NCORES):
        b, half = c // 2, c % 2
        t0 = 0 if half == 0 else SEQ - NP
        cnt = np.zeros((128, 4, 16), np.float32)
        for g, k in enumerate((2, 4, 8, 16)):
            for t in range(16):
                cnt[:, g, t] = 1.0 / min(t0 + t + 1, k)
        sl = slice(16 * c, 16 * c + 16)
        m = {
            "xp": np.ascontiguousarray(inp["x_prompt"][b, t0:t0 + NP]),
            "xs": _jmajor(inp["x_sample"][sl]),
            "st_pool": np.stack([_jmajor(inp["state_pool"][l, sl]) for l in range(DEPTH)]),
            "st_sconv": np.stack([_jmajor(inp["state_sconv"][l, sl]) for l in range(DEPTH)]),
            "st_cconv": np.stack([_jmajor(inp["state_cconv"][l, sl]) for l in range(DEPTH)]),
            "st_ffn": np.stack([_jmajor(inp["state_ffn"][l, sl]) for l in range(DEPTH)]),
            "vecs": vecs,
            "cnt": cnt.reshape(128, 64),
            "ident": ident,
            "w_in": inp["w_in"],
            "pool_w": inp["pool_w"],
            "w_out": inp["w_out"],
            "w_up": inp["w_up"],
            "w_down": inp["w_down"],
        }
        in_maps.append(m)
    res = run_bass_kernel_spmd(nc, in_maps, core_ids=list(range(NCORES)))
    R = res.results
    B = 4
    y_prompt = np.zeros((B, SEQ, D), np.float32)
    y_sample = np.zeros((128, 8, D), np.float32)
    new_pool_p = np.zeros((DEPTH, B, 15, D_POOL), np.float32)
    new_sconv_p = np.zeros((DEPTH, B, 2, D_SC), np.float32)
    new_cconv_p = np.zeros((DEPTH, B, 30, D_CC), np.float32)
    new_ffn_p = np.zeros((DEPTH, B, 2, 2 * D_FF), np.float32)
    new_pool_s = np.zeros((DEPTH, 128, 15, D_POOL), np.float32)
    new_sconv_s = np.zeros((DEPTH, 128, 2, D_SC), np.float32)
    new_cconv_s = np.zeros((DEPTH, 128, 30, D_CC), np.float32)
    new_ffn_s = np.zeros((DEPTH, 128, 2, 2 * D_FF), np.float32)
    for c in range(NCORES):
        b, half = c // 2, c % 2
        r = R[c]
        sl = slice(16 * c, 16 * c + 16)
        if half == 0:
            y_prompt[b, 0:NP] = r["yp"]
        else:
            y_prompt[b, NP:SEQ] = r["yp"][2 * NP - SEQ:]
            for l in range(DEPTH):
                new_pool_p[l, b] = r["o_pool_p"][l]
                new_sconv_p[l, b] = r["o_sconv_p"][l]
                new_cconv_p[l, b] = r["o_cconv_p"][l]
                new_ffn_p[l, b] = r["o_ffn_p"][l]
        y_sample[sl] = _unjmajor(r["ys"], 8)
        for l in range(DEPTH):
            new_pool_s[l, sl] = _unjmajor(r["o_pool_s"][l], 15)
            new_sconv_s[l, sl] = _unjmajor(r["o_sconv_s"][l], 2)
            new_cconv_s[l, sl] = _unjmajor(r["o_cconv_s"][l], 30)
            new_ffn_s[l, sl] = _unjmajor(r["o_ffn_s"][l], 2)
    return (y_prompt, y_sample, new_pool_p, new_sconv_p, new_cconv_p, new_ffn_p,
            new_pool_s, new_sconv_s, new_cconv_s, new_ffn_s)
```

```python
from contextlib import ExitStack
import numpy as np
import concourse.bass as bass
import concourse.mybir as mybir
from concourse.bass_utils import run_bass_kernel_spmd

F32 = mybir.dt.float32
BF16 = mybir.dt.bfloat16
AF = mybir.ActivationFunctionType
ALU = mybir.AluOpType

D = 2048
DEPTH = 2
SEQ = 2048
D_POOL, D_SC, D_CC = 512, 768, 768
D_FF = 5632
D_IN = 4352
EPS = 1e-6
NP = 1056
NS = 128
T = NP + NS
BLKS = [(0, 512), (512, 512), (1024, 160)]
NCORES = 8
WSLOTS = 5
GF = 6
ENGS = ("pe", "act", "dve", "pool", "sp")


class Op:
    __slots__ = ("eng", "fn", "deps", "idx", "milestone", "mnum", "is_dma",
                 "dma_sem", "dma_val", "pre_dma_wait")

    def __init__(self, eng, fn):
        self.eng = eng
        self.fn = fn
        self.deps = []
        self.idx = -1
        self.milestone = False
        self.mnum = 0
        self.is_dma = False
        self.dma_sem = None
        self.dma_val = 0
        self.pre_dma_wait = None


class Sched:
    def __init__(self, n_dma_sems=12):
        self.ops = {e: [] for e in ENGS}
        self.res_w = {}
        self.res_r = {}
        self.n_dma_sems = n_dma_sems
        self.dma_count = {e: 0 for e in ENGS}

    def op(self, eng, fn, reads=(), writes=(), dma=False):
        o = Op(eng, fn)
        best = {}
        dmadeps = {}

        def add(d):
            if d is None:
                return
            if d.is_dma:
                dmadeps[id(d)] = d
            else:
                b = best.get(d.eng)
                if b is None or d.idx > b.idx:
                    best[d.eng] = d

        for r in reads:
            add(self.res_w.get(r))
        for r in writes:
            add(self.res_w.get(r))
            rr = self.res_r.get(r)
            if rr:
                for rd in rr.values():
                    add(rd)
        o.deps = list(best.values()) + list(dmadeps.values())
        o.idx = len(self.ops[eng])
        o.is_dma = dma
        for r in reads:
            rr = self.res_r.setdefault(r, {})
            rr[("d", id(o)) if dma else eng] = o
        for r in writes:
            self.res_w[r] = o
            self.res_r[r] = {}
        self.ops[eng].append(o)
        if dma:
            n = self.dma_count[eng]
            self.dma_count[eng] = n + 1
            k = n % self.n_dma_sems
            o.dma_sem = (eng, k)
            o.dma_val = 16 * (n // self.n_dma_sems + 1)
            if n >= self.n_dma_sems:
                o.pre_dma_wait = ((eng, k), 16 * (n // self.n_dma_sems))
        return o

    def emit(self, block_fns, sems, dma_sems):
        plans = {}
        for e in ENGS:
            known = {f: -1 for f in ENGS}
            known_dma = {}
            plan = []
            for o in self.ops[e]:
                waits = []
                if o.pre_dma_wait is not None:
                    s, v = o.pre_dma_wait
                    if known_dma.get(s, 0) < v:
                        known_dma[s] = v
                        waits.append(("dma", s, v))
                for d in o.deps:
                    if d.is_dma:
                        if known_dma.get(d.dma_sem, 0) < d.dma_val:
                            known_dma[d.dma_sem] = d.dma_val
                            waits.append(("dma", d.dma_sem, d.dma_val))
                    else:
                        if d.eng == e and e == "pe":
                            continue
                        if known[d.eng] >= d.idx:
                            continue
                        known[d.eng] = d.idx
                        d.milestone = True
                        waits.append(("eng", d))
                plan.append(waits)
            plans[e] = plan
        for e in ENGS:
            m = 0
            for o in self.ops[e]:
                if o.milestone:
                    m += 1
                    o.mnum = m
        final_dma = {}
        for e in ENGS:
            for o in self.ops[e]:
                if o.is_dma:
                    final_dma[o.dma_sem] = max(final_dma.get(o.dma_sem, 0), o.dma_val)

        def make(e):
            def body(eng):
                for o, waits in zip(self.ops[e], plans[e]):
                    for w in waits:
                        if w[0] == "dma":
                            eng.wait_ge(dma_sems[w[1]], w[2])
                        else:
                            eng.wait_ge(sems[w[1].eng], w[1].mnum)
                    ins = o.fn(eng)
                    if o.is_dma:
                        ins.then_inc(dma_sems[o.dma_sem], 16)
                    elif o.milestone:
                        ins.then_inc(sems[e], 1)
                if e == "sp":
                    for s, v in sorted(final_dma.items()):
                        eng.wait_ge(dma_sems[s], v)
            return body

        for e in ENGS:
            block_fns[e](make(e))


def _vec_layout():
    off = {}
    n = 0

    def add(name, c):
        nonlocal n
        off[name] = n
        n += c

    for l in range(DEPTH):
        add(("n1g", l), 16)
        add(("n2g", l), 16)
        add(("psc", l), 4)
        for t in range(3):
            add(("sw", l, t), 6)
        for t in range(31):
            add(("cw", l, t), 6)
        add(("cb", l), 6)
        add(("cg", l), 6)
        for t in range(3):
            add(("fw", l, t), 88)
    add(("fng",), 16)
    return off, n


VOFF, NV = _vec_layout()


def _pack_vecs(inp):
    v = np.zeros((128, NV), np.float32)

    def put(name, arr):
        a = np.asarray(arr, np.float32).reshape(-1, 128).T
        v[:, VOFF[name]:VOFF[name] + a.shape[1]] = a

    for l in range(DEPTH):
        put(("n1g", l), inp["norm1_g"][l])
        put(("n2g", l), inp["norm2_g"][l])
        put(("psc", l), inp["pool_scale"][l])
        for t in range(3):
            put(("sw", l, t), inp["sconv_w"][l, t])
        for t in range(31):
            put(("cw", l, t), inp["cconv_w"][l, t])
        put(("cb", l), inp["cconv_b"][l])
        put(("cg", l), inp["cconv_norm_g"][l])
        for t in range(3):
            put(("fw", l, t), inp["ffn_conv_w"][l, t])
    put(("fng",), inp["final_norm_g"])
    return v


_NC_CACHE = {}


def build_program():
    nc = bass.Bass("TRN2", target_bir_lowering=False)

    def din(name, shape):
        return nc.dram_tensor(name, list(shape), F32, kind="ExternalInput").ap()

    def dout(name, shape):
        return nc.dram_tensor(name, list(shape), F32, kind="ExternalOutput").ap()

    xp_d = din("xp", [NP, D])
    xs_d = din("xs", [NS, D])
    stp_d = din("st_pool", [DEPTH, 240, D_POOL])
    sts_d = din("st_sconv", [DEPTH, 32, D_SC])
    stc_d = din("st_cconv", [DEPTH, 480, D_CC])
    stf_d = din("st_ffn", [DEPTH, 32, 2 * D_FF])
    vecs_d = din("vecs", [128, NV])
    cnt_d = din("cnt", [128, 4 * 16])
    ident_d = din("ident", [128, 128])
    w_in_d = din("w_in", [DEPTH, D, D_IN])
    pool_w_d = din("pool_w", [DEPTH, 4, 128, 128])
    w_out_d = din("w_out", [DEPTH, D, D])
    w_up_d = din("w_up", [DEPTH, D, 2 * D_FF])
    w_down_d = din("w_down", [DEPTH, D_FF, D])

    yp_d = dout("yp", [NP, D])
    ys_d = dout("ys", [NS, D])
    o_pool_p = dout("o_pool_p", [DEPTH, 15, D_POOL])
    o_sconv_p = dout("o_sconv_p", [DEPTH, 2, D_SC])
    o_cconv_p = dout("o_cconv_p", [DEPTH, 30, D_CC])
    o_ffn_p = dout("o_ffn_p", [DEPTH, 2, 2 * D_FF])
    o_pool_s = dout("o_pool_s", [DEPTH, 240, D_POOL])
    o_sconv_s = dout("o_sconv_s", [DEPTH, 32, D_SC])
    o_cconv_s = dout("o_cconv_s", [DEPTH, 480, D_CC])
    o_ffn_s = dout("o_ffn_s", [DEPTH, 32, 2 * D_FF])

    S = Sched()
    es = ExitStack()

    def sb(name, shape, dt):
        return es.enter_context(nc.sbuf_tensor("sb_" + name, shape, dt))

    xT = sb("xT", [128, 16, T], F32)
    hT = sb("hT", [128, 16, T], BF16)
    NHALF = 2 * WSLOTS
    wr = sb("wr", [128, NHALF * 8, 128], BF16)
    vecs = sb("vecs", [128, NV], F32)
    rstd = sb("rstd", [128, T], F32)
    sqb = [sb(f"sq{i}", [128, T], BF16) for i in range(2)]
    identF = sb("identF", [128, 128], F32)
    onesB = sb("onesB", [128, 128], BF16)
    epsT = sb("epsT", [128, 1], F32)
    identB = sb("identB", [128, 128], BF16)
    dg = sb("dg", [128, 4, 128], BF16)
    dg_rot = {"i": 0}
    cnt = sb("cnt", [128, 4, 16], F32)
    poolw = sb("poolw", [128, 4, 128], BF16)
    CC_W = 6 * T
    EW = 1704
    ARENA_W = CC_W + 3 * EW + 2 * T + 512 + 256
    arena = sb("arena", [128, ARENA_W], F32)
    cc = arena[:, 0:CC_W]
    cc_bf = cc.bitcast(BF16)
    Ebuf = [arena[:, CC_W + k * EW: CC_W + (k + 1) * EW] for k in range(3)]
    o_t = CC_W + 3 * EW
    tbuf = [arena[:, o_t + k * T: o_t + (k + 1) * T] for k in range(2)]
    o_s = o_t + 2 * T
    stg = arena[:, o_s:o_s + 512]
    tailstg = arena[:, o_s + 512:o_s + 768]

    pA = es.enter_context(nc.psum_tensor("pA", [128, 1536], F32))
    pB = es.enter_context(nc.psum_tensor("pB", [128, 1536], F32))
    pC = es.enter_context(nc.psum_tensor("pC", [128, 1024], F32))
    PS = {"pA": pA, "pB": pB}

    sems = {e: es.enter_context(nc.semaphore(f"s_{e}")) for e in ENGS}
    dsems = {}
    for e in ("sp", "pool", "act"):
        for k in range(S.n_dma_sems):
            dsems[(e, k)] = es.enter_context(nc.semaphore(f"d_{e}{k}"))
    block = es.enter_context(nc.Block())

    def V(name, c=0, n=1):
        o = VOFF[name] + c
        return vecs[:, o:o + n]

    def ccres_bf(m):
        return ("cc", m // 2)

    def mixbf(m):
        return cc_bf[:, m * T:(m + 1) * T]

    ps_state = {"i": 0}

    def next_ps():
        n = ("pA", "pB")[ps_state["i"] % 2]
        ps_state["i"] += 1
        return n

    evac_state = {"i": 0}

    wstate = {"i": 0}

    def wload(src_ap, nk):
        p = wstate["i"]
        if nk > 8:
            if p % 2 == 1:
                p += 1
            hs = [p % NHALF, (p + 1) % NHALF]
            wstate["i"] = p + 2
        else:
            hs = [p % NHALF]
            wstate["i"] = p + 1
        base = hs[0] * 8
        res = [("w", h) for h in hs]
        S.op("pool", lambda e: e.dma_start(out=wr[:, base:base + nk, :], in_=src_ap),
             writes=res, dma=True)
        return base, res

    def wcols(wd, l, rows0, nk, col0):
        return wd[l, rows0:rows0 + nk * 128, col0:col0 + 128].rearrange("(kc p) c -> p kc c", p=128)

    def mm_group(psn, lhs_fn, rhs_fn, nk, reads, kreads=None):
        ps = PS[psn]
        for k in range(nk):
            rk = reads if kreads is None else reads + kreads(k)
            for (c0, n) in BLKS:
                S.op("pe", lambda e, k=k, c0=c0, n=n: e.matmul(
                    ps[:, c0:c0 + n], lhsT=lhs_fn(k), rhs=rhs_fn(k)[:, c0:c0 + n],
                    start=(k == 0), stop=(k == nk - 1)),
                    reads=rk, writes=[psn])

    def proj_chunk(wd, l, col0, src_fn, src_res, nk=16, rows0=0):
        base, wres = wload(wcols(wd, l, rows0, nk, col0), nk)
        psn = next_ps()
        pend = deferred[:]
        deferred.clear()
        if callable(src_res):
            mm_group(psn, lambda k: wr[:, base + k, :], src_fn, nk, reads=wres, kreads=src_res)
        else:
            mm_group(psn, lambda k: wr[:, base + k, :], src_fn, nk, reads=wres + src_res)
        for f in pend:
            f()
        return psn

    deferred = []

    def flush_deferred():
        pend = deferred[:]
        deferred.clear()
        for f in pend:
            f()

    def norm_stats(src_fn, src_res, nch, inv_n, bias_fn=None):
        psn = next_ps()
        ps = PS[psn]
        for c in range(nch):
            sq = sqb[c % 2]
            sqr = ("sq", c % 2)
            if bias_fn is None and c % 2 == 1:
                S.op("dve", lambda e, c=c, sq=sq: e.tensor_tensor(out=sq[:], in0=src_fn(c), in1=src_fn(c), op=ALU.mult),
                     reads=src_res(c), writes=[sqr])
            elif bias_fn is None:
                S.op("act", lambda e, c=c, sq=sq: e.activation(out=sq[:], in_=src_fn(c), func=AF.Square),
                     reads=src_res(c), writes=[sqr])
            else:
                S.op("act", lambda e, c=c, sq=sq: e.activation(out=sq[:], in_=src_fn(c), func=AF.Square, bias=bias_fn(c)),
                     reads=src_res(c), writes=[sqr])
            for (c0, n) in BLKS:
                S.op("pe", lambda e, c=c, c0=c0, n=n, sq=sq: e.matmul(
                    ps[:, c0:c0 + n], lhsT=onesB[:], rhs=sq[:, c0:c0 + n],
                    start=(c == 0), stop=(c == nch - 1)),
                    reads=[sqr, "onesB"], writes=[psn])
        S.op("act", lambda e: e.activation(out=rstd[:], in_=ps[:, 0:T], func=AF.Sqrt, bias=epsT[:], scale=inv_n),
             reads=[psn, "epsT"], writes=["rstd"])
        S.op("dve", lambda e: e.reciprocal(out=rstd[:], in_=rstd[:]),
             reads=["rstd"], writes=["rstd"])

    def rmsnorm_to_h(gname):
        norm_stats(lambda c: xT[:, c, :], lambda c: [("x", c)], 16, 1.0 / D)
        for c in range(16):
            S.op("dve", lambda e, c=c: e.scalar_tensor_tensor(
                out=hT[:, c, :], in0=xT[:, c, :], scalar=V(gname, c), in1=rstd[:],
                op0=ALU.mult, op1=ALU.mult),
                reads=[("x", c), "rstd", "vecs"], writes=[("h", c)])

    H_RES = lambda k: [("h", k)]

    cstate = {"i": 0}

    def next_c():
        k = cstate["i"] % 2
        cstate["i"] += 1
        return k

    def load_state_T(src_rows_fn, nrows_total, dst_ap_fn):
        ntile = (nrows_total + 119) // 120
        rows = []
        r0 = 0
        for i in range(ntile):
            n = min(120, nrows_total - r0)
            rows.append((r0, n))
            r0 += n
        ck = next_c()
        cres = ("pC", ck)
        if ntile == 1:
            tl = [stg_rot["i"] % 4]
            stg_rot["i"] += 1
        else:
            tl = list(range(ntile))
        for i, (r0, n) in enumerate(rows):
            ti = tl[i]
            S.op("sp", lambda e, ti=ti, r0=r0, n=n: e.dma_start(out=stg[0:n, ti * 128:(ti + 1) * 128], in_=src_rows_fn(r0, n)),
                 writes=[("stg", ti)], dma=True)
        for i, (r0, n) in enumerate(rows):
            ti = tl[i]
            S.op("pe", lambda e, ti=ti, r0=r0, n=n: e.transpose(
                out=pC[:, ck * 512 + r0: ck * 512 + r0 + n], in_=stg[0:n, ti * 128:(ti + 1) * 128],
                identity=identF[0:n, 0:n]),
                reads=[("stg", ti), "identF"], writes=[cres])
        return ck, cres

    stg_rot = {"i": 0}

    tstate = {"i": 0}

    def store_tail(src_ap, ncols, src_res, dst_ap):
        deferred.append(lambda: _store_tail(src_ap, ncols, src_res, dst_ap))

    def _store_tail(src_ap, ncols, src_res, dst_ap):
        ck = next_c()
        cres = ("pC", ck)
        S.op("pe", lambda e: e.transpose(out=pC[0:ncols, ck * 512: ck * 512 + 128], in_=src_ap, identity=identF[:]),
             reads=list(src_res) + ["identF"], writes=[cres])
        k = tstate["i"] % 2
        tstate["i"] += 1
        ts = tailstg[:, k * 128:(k + 1) * 128]
        S.op("act", lambda e: e.activation(out=ts[0:ncols, :], in_=pC[0:ncols, ck * 512: ck * 512 + 128], func=AF.Copy),
             reads=[cres], writes=[("tail", k)])
        S.op("act", lambda e: e.dma_start(out=dst_ap, in_=ts[0:ncols, :]), reads=[("tail", k)], dma=True)

    S.op("sp", lambda e: e.dma_start(out=vecs[:], in_=vecs_d), writes=["vecs"], dma=True)
    S.op("sp", lambda e: e.dma_start(out=identF[:], in_=ident_d), writes=["identF"], dma=True)
    S.op("sp", lambda e: e.dma_start(out=cnt[:], in_=cnt_d.rearrange("p (a b) -> p a b", a=4)), writes=["cnt"], dma=True)
    S.op("dve", lambda e: e.memset(onesB[:], 1.0), writes=["onesB"])
    S.op("dve", lambda e: e.memset(epsT[:], EPS), writes=["epsT"])
    S.op("dve", lambda e: e.tensor_copy(out=identB[:], in_=identF[:]), reads=["identF"], writes=["identB"])
    for k in range(3):
        S.op("dve", lambda e, k=k: e.memset(Ebuf[k], 0.0), writes=[("E", k)])

    xin = [arena[:, 0:2048], arena[:, 2368:2368 + 2048], arena[:, 4736:4736 + 2048]]
    xin_res = [[("cc", 0), ("cc", 1)], [("cc", 2), ("cc", 3)], [("cc", 4), ("cc", 5)]]
    tiles = [(xp_d, i * 128, 128, i * 128) for i in range(8)] + [(xp_d, 1024, 32, 1024), (xs_d, 0, 128, NP)]
    for ti, (src, r0, n, col0) in enumerate(tiles):
        xi = xin[ti % 3]
        xr = xin_res[ti % 3]
        S.op("sp", lambda e, xi=xi, src=src, r0=r0, n=n: e.dma_start(out=xi[0:n, :], in_=src[r0:r0 + n, :]),
             writes=xr, dma=True)
        for half in range(2):
            psn = next_ps()
            ps = PS[psn]
            for j in range(8):
                kc = half * 8 + j
                S.op("pe", lambda e, ps=ps, j=j, kc=kc, xi=xi, n=n: e.transpose(
                    out=ps[:, j * 128: j * 128 + n], in_=xi[0:n, kc * 128:(kc + 1) * 128], identity=identF[0:n, 0:n]),
                    reads=xr + ["identF"], writes=[psn])
            src_v = ps[:, 0:1024].rearrange("p (a b) -> p a b", a=8)[:, :, 0:n]
            dst_v = xT[:, half * 8:(half + 1) * 8, col0:col0 + n]
            xres = [("x", c) for c in range(half * 8, half * 8 + 8)]
            if half == 0:
                S.op("act", lambda e, dst_v=dst_v, src_v=src_v: e.activation(out=dst_v, in_=src_v, func=AF.Copy),
                     reads=[psn], writes=xres)
            else:
                S.op("dve", lambda e, dst_v=dst_v, src_v=src_v: e.tensor_copy(out=dst_v, in_=src_v),
                     reads=[psn], writes=xres)

    def conv_taps(dst, ext, p_off, s_off, wname, l, ci, ntap, ext_res, dst_res, bias=None):
        for (d0, dn, e0, sh) in ((0, NP, p_off, 1), (NP, NS, s_off, 16)):
            for i in range(ntap):
                src = ext[:, e0 + i * sh: e0 + i * sh + dn]
                w = V((wname, l, i), ci)
                if i == 0:
                    if bias is None:
                        S.op("dve", lambda e, src=src, w=w, d0=d0, dn=dn: e.tensor_scalar(
                            out=dst[:, d0:d0 + dn], in0=src, scalar1=w, scalar2=None, op0=ALU.mult),
                            reads=ext_res + ["vecs"], writes=dst_res)
                    else:
                        S.op("dve", lambda e, src=src, w=w, d0=d0, dn=dn: e.tensor_scalar(
                            out=dst[:, d0:d0 + dn], in0=src, scalar1=w, scalar2=bias, op0=ALU.mult, op1=ALU.add),
                            reads=ext_res + ["vecs"], writes=dst_res)
                else:
                    S.op("dve", lambda e, src=src, w=w, d0=d0, dn=dn: e.scalar_tensor_tensor(
                        out=dst[:, d0:d0 + dn], in0=src, scalar=w, in1=dst[:, d0:d0 + dn],
                        op0=ALU.mult, op1=ALU.add),
                        reads=ext_res + ["vecs"] + dst_res, writes=dst_res)

    def out_proj_partial(l, mix_fn, mix_res, kc0, nk):
        for dc in range(16):
            psn = proj_chunk(w_out_d, l, dc * 128, mix_fn, mix_res, nk=nk, rows0=kc0 * 128)
            ps = PS[psn]
            S.op("dve", lambda e, ps=ps, dc=dc: e.tensor_tensor(out=xT[:, dc, :], in0=ps[:, 0:T], in1=xT[:, dc, :], op=ALU.add),
                 reads=[psn, ("x", dc)], writes=[("x", dc)])

    def zero_halos():
        for k in range(3):
            S.op("dve", lambda e, k=k: e.memset(Ebuf[k][:, 0:32], 0.0), writes=[("E", k), ("Es", k)])

    ES0 = 1218
    stT = [Ebuf[0][:, ES0:ES0 + 384].rearrange("p (a b) -> p a b", b=32), Ebuf[1][:, ES0:ES0 + 384].rearrange("p (a b) -> p a b", b=32)]
    oT3 = Ebuf[2][:, ES0:ES0 + 384].rearrange("p (a b) -> p a b", b=32)
    oP3 = stg[:, 0:384].rearrange("p (a b) -> p a b", b=32)
    STG_ALL = [("stg", i) for i in range(4)]

    for l in range(DEPTH):
        rmsnorm_to_h(("n1g", l))
        S.op("pool", lambda e, l=l: e.dma_start(out=poolw[:], in_=pool_w_d[l].rearrange("g c d -> c g d")),
             writes=["poolw"], dma=True)

        P_OFF, PS_OFF = 15, 15 + NP
        zero_halos()
        for g in range(4):
            kwin = (2, 4, 8, 16)[g]
            E0, E1, E2 = Ebuf[0], Ebuf[1], Ebuf[2]
            psn = proj_chunk(w_in_d, l, g * 128, lambda k: hT[:, k, :], H_RES)
            ps = PS[psn]
            ck, cres = load_state_T(lambda r0, n, g=g, l=l: stp_d[l, r0:r0 + n, g * 128:(g + 1) * 128], 240, None)
            S.op("act", lambda e, ck=ck: e.activation(out=E0[:, PS_OFF:PS_OFF + 240], in_=pC[:, ck * 512: ck * 512 + 240], func=AF.Copy),
                 reads=[cres], writes=[("E", 0)])
            S.op("act", lambda e, ps=ps: e.activation(out=E0[:, P_OFF:P_OFF + NP], in_=ps[:, 0:NP], func=AF.Copy),
                 reads=[psn], writes=[("E", 0)])
            S.op("act", lambda e, ps=ps: e.activation(out=E0[:, PS_OFF + 240:PS_OFF + 368], in_=ps[:, NP:T], func=AF.Copy),
                 reads=[psn], writes=[("E", 0)])
            store_tail(E0[:, P_OFF + NP - 15:P_OFF + NP], 15, [("E", 0)], o_pool_p[l, :, g * 128:(g + 1) * 128])
            store_tail(E0[:, PS_OFF + 240:PS_OFF + 368], 128, [("E", 0)], o_pool_s[l, 7 * 16:15 * 16, g * 128:(g + 1) * 128])
            cur, cur_r = E0, ("E", 0)
            others = [(E1, ("E", 1)), (E2, ("E", 2))]
            sh = 1
            step = 0
            while sh < kwin:
                nxt, nxt_r = others[step % 2]
                S.op("dve", lambda e, cur=cur, nxt=nxt, sh=sh: e.tensor_tensor(
                    out=nxt[:, sh:P_OFF + NP], in0=cur[:, sh:P_OFF + NP], in1=cur[:, 0:P_OFF + NP - sh], op=ALU.add),
                    reads=[cur_r], writes=[nxt_r])
                S.op("dve", lambda e, cur=cur, nxt=nxt, sh=sh: e.tensor_tensor(
                    out=nxt[:, PS_OFF + sh * 16:PS_OFF + 368], in0=cur[:, PS_OFF + sh * 16:PS_OFF + 368],
                    in1=cur[:, PS_OFF:PS_OFF + 368 - sh * 16], op=ALU.add),
                    reads=[cur_r], writes=[nxt_r])
                cur, cur_r = nxt, nxt_r
                sh *= 2
                step += 1
            dbf = tbuf[0].bitcast(BF16)[:, (g % 2) * T:(g % 2 + 1) * T]
            tdr = ("td", g % 2)
            inv = 1.0 / kwin
            S.op("dve", lambda e, cur=cur, inv=inv, dbf=dbf: e.scalar_tensor_tensor(
                out=dbf[:, 0:NP], in0=cur[:, P_OFF:P_OFF + NP], scalar=inv, in1=E0[:, P_OFF:P_OFF + NP],
                op0=ALU.mult, op1=ALU.subtract),
                reads=[cur_r, ("E", 0)], writes=[tdr])
            S.op("dve", lambda e, cur=cur, inv=inv, dbf=dbf: e.scalar_tensor_tensor(
                out=dbf[:, NP:T], in0=cur[:, PS_OFF + 240:PS_OFF + 368], scalar=inv, in1=E0[:, PS_OFF + 240:PS_OFF + 368],
                op0=ALU.mult, op1=ALU.subtract),
                reads=[cur_r, ("E", 0)], writes=[tdr])
            f16 = tbuf[1][:, 0:16]
            S.op("dve", lambda e, cur=cur, g=g: e.tensor_tensor(out=f16, in0=cur[:, P_OFF:P_OFF + 16], in1=cnt[:, g, :], op=ALU.mult),
                 reads=[cur_r, "cnt"], writes=[("t", 1)])
            S.op("dve", lambda e, dbf=dbf: e.tensor_tensor(out=dbf[:, 0:16], in0=f16, in1=E0[:, P_OFF:P_OFF + 16], op=ALU.subtract),
                 reads=[("t", 1), ("E", 0), tdr], writes=[tdr])

            def pool_mm(g=g, l=l, dbf=dbf, tdr=tdr):
                psn2 = next_ps()
                ps2 = PS[psn2]
                for (c0, n) in BLKS:
                    S.op("pe", lambda e, ps2=ps2, c0=c0, n=n, g=g, dbf=dbf: e.matmul(ps2[:, c0:c0 + n], lhsT=poolw[:, g, :], rhs=dbf[:, c0:c0 + n], start=True, stop=True),
                         reads=["poolw", tdr], writes=[psn2])
                S.op("act", lambda e, ps2=ps2, g=g, l=l: e.activation(out=mixbf(g), in_=ps2[:, 0:T], func=AF.Copy, scale=V(("psc", l), g)),
                     reads=[psn2, "vecs"], writes=[ccres_bf(g)])
            deferred.append(pool_mm)
        flush_deferred()
        out_proj_partial(l, lambda k: mixbf(k), lambda k: [ccres_bf(k)], 0, 4)
        zero_halos()

        o1 = D_POOL
        SP_OFF, SS_OFF = 2, 2 + NP
        for i in range(6):
            Ei, Er = Ebuf[i % 2], ("E", i % 2)
            gc, gcr = tbuf[0], ("t", 0)
            cbuf, cbr = tbuf[1], ("t", 1)
            psn = proj_chunk(w_in_d, l, o1 + D_SC + i * 128, lambda k: hT[:, k, :], H_RES)
            ps = PS[psn]
            ck, cres = load_state_T(lambda r0, n, i=i, l=l: sts_d[l, r0:r0 + n, i * 128:(i + 1) * 128], 32, None)
            S.op("act", lambda e, ck=ck, Ei=Ei: e.activation(out=Ei[:, SS_OFF:SS_OFF + 32], in_=pC[:, ck * 512: ck * 512 + 32], func=AF.Copy),
                 reads=[cres], writes=[Er])
            S.op("act", lambda e, ps=ps: e.activation(out=gc[:, 0:T], in_=ps[:, 0:T], func=AF.Copy),
                 reads=[psn], writes=[gcr])
            psn = proj_chunk(w_in_d, l, o1 + 2 * D_SC + i * 128, lambda k: hT[:, k, :], H_RES)
            ps = PS[psn]
            S.op("dve", lambda e, ps=ps, Ei=Ei: e.tensor_tensor(out=Ei[:, SP_OFF:SP_OFF + NP], in0=ps[:, 0:NP], in1=gc[:, 0:NP], op=ALU.mult),
                 reads=[psn, gcr], writes=[Er])
            S.op("dve", lambda e, ps=ps, Ei=Ei: e.tensor_tensor(out=Ei[:, SS_OFF + 32:SS_OFF + 160], in0=ps[:, NP:T], in1=gc[:, NP:T], op=ALU.mult),
                 reads=[psn, gcr], writes=[Er])
            store_tail(Ei[:, SP_OFF + NP - 2:SP_OFF + NP], 2, [Er], o_sconv_p[l, :, i * 128:(i + 1) * 128])
            store_tail(Ei[:, SS_OFF + 32 + 96:SS_OFF + 160], 32, [Er], o_sconv_s[l, :, i * 128:(i + 1) * 128])
            conv_taps(cbuf, Ei, 0, SS_OFF, "sw", l, i, 3, [Er], [cbr])
            psn = proj_chunk(w_in_d, l, o1 + i * 128, lambda k: hT[:, k, :], H_RES)
            ps = PS[psn]
            S.op("dve", lambda e, ps=ps, i=i: e.tensor_tensor(out=mixbf(i), in0=ps[:, 0:T], in1=cbuf[:, 0:T], op=ALU.mult),
                 reads=[psn, cbr], writes=[ccres_bf(i)])
        out_proj_partial(l, lambda k: mixbf(k), lambda k: [ccres_bf(k)], 4, 6)

        o4 = D_POOL + 3 * D_SC
        CP_OFF, CS_OFF = 30, 30 + NP
        zero_halos()
        VT_P, VS = 1000, 1040
        pending_conv = []

        def cconv_pe(i, Ei, Er, l=l):
            Ebf = Ei.bitcast(BF16)
            psn = next_ps()
            ps = PS[psn]
            for t in range(31):
                r = dg_rot["i"] % 4
                dg_rot["i"] += 1
                S.op("dve", lambda e, r=r, t=t: e.tensor_scalar(out=dg[:, r, :], in0=identB[:], scalar1=V(("cw", l, t), i), scalar2=None, op0=ALU.mult),
                     reads=["identB", "vecs"], writes=[("dg", r)])
                for (c0, n, src0) in ((0, 512, t), (512, 512, t + 512), (1024, 32, t + 1024)):
                    S.op("pe", lambda e, r=r, t=t, c0=c0, n=n, src0=src0: e.matmul(
                        ps[:, c0:c0 + n], lhsT=dg[:, r, :], rhs=Ebf[:, src0:src0 + n], start=(t == 0), stop=(t == 30)),
                        reads=[("dg", r), Er], writes=[psn])
            S.op("act", lambda e: e.activation(out=cc[:, i * T:i * T + NP], in_=ps[:, 0:NP], func=AF.Identity, bias=V(("cb", l), i)),
                 reads=[psn, "vecs"], writes=[("cc", i)])
            dst = cc[:, i * T + NP:(i + 1) * T]
            for t in range(31):
                src = Ei[:, VS + t * 16: VS + t * 16 + NS]
                w = V(("cw", l, t), i)
                if t == 0:
                    S.op("dve", lambda e, src=src, w=w: e.tensor_scalar(out=dst, in0=src, scalar1=w, scalar2=V(("cb", l), i), op0=ALU.mult, op1=ALU.add),
                         reads=[Er, "vecs"], writes=[("ccs", i)])
                else:
                    S.op("dve", lambda e, src=src, w=w: e.scalar_tensor_tensor(out=dst, in0=src, scalar=w, in1=dst, op0=ALU.mult, op1=ALU.add),
                         reads=[Er, "vecs", ("ccs", i)], writes=[("ccs", i)])

        for i in range(6):
            Ei, Er = Ebuf[i % 2], ("E", i % 2)
            Ebf = Ei.bitcast(BF16)
            sg, sgr = tbuf[0], ("t", 0)
            psn = proj_chunk(w_in_d, l, o4 + D_CC + i * 128, lambda k: hT[:, k, :], H_RES)
            ps = PS[psn]
            ck, cres = load_state_T(lambda r0, n, i=i, l=l: stc_d[l, r0:r0 + n, i * 128:(i + 1) * 128], 480, None)
            S.op("act", lambda e, ck=ck, Ei=Ei: e.activation(out=Ei[:, VS:VS + 480], in_=pC[:, ck * 512: ck * 512 + 480], func=AF.Copy),
                 reads=[cres], writes=[Er])
            for f in pending_conv:
                f()
            pending_conv.clear()
            S.op("act", lambda e, ps=ps: e.activation(out=sg[:, 0:T], in_=ps[:, 0:T], func=AF.Sigmoid),
                 reads=[psn], writes=[sgr])
            psn = proj_chunk(w_in_d, l, o4 + i * 128, lambda k: hT[:, k, :], H_RES)
            ps = PS[psn]
            S.op("dve", lambda e, ps=ps, Ebf=Ebf: e.tensor_tensor(out=Ebf[:, CP_OFF:CP_OFF + NP], in0=ps[:, 0:NP], in1=sg[:, 0:NP], op=ALU.mult),
                 reads=[psn, sgr], writes=[Er])
            S.op("dve", lambda e, ps=ps, Ei=Ei: e.tensor_tensor(out=Ei[:, VS + 480:VS + 608], in0=ps[:, NP:T], in1=sg[:, NP:T], op=ALU.mult),
                 reads=[psn, sgr], writes=[Er])
            S.op("dve", lambda e, ps=ps, Ei=Ei: e.tensor_tensor(out=Ei[:, VT_P:VT_P + 30], in0=ps[:, NP - 30:NP], in1=sg[:, NP - 30:NP], op=ALU.mult),
                 reads=[psn, sgr], writes=[Er])
            store_tail(Ei[:, VT_P:VT_P + 30], 30, [Er], o_cconv_p[l, :, i * 128:(i + 1) * 128])
            store_tail(Ei[:, VS + 480:VS + 608], 128, [Er], o_cconv_s[l, 22 * 16:30 * 16, i * 128:(i + 1) * 128])
            pending_conv.append(lambda i=i, Ei=Ei, Er=Er: cconv_pe(i, Ei, Er))
        for f in pending_conv:
            f()
        pending_conv.clear()
        norm_stats(lambda c: cc[:, c * T:(c + 1) * T], lambda c: [("cc", c), ("ccs", c)], 6, 1.0 / D_CC)
        for i in range(6):
            yt, ytr = tbuf[i % 2], ("t", i % 2)
            S.op("dve", lambda e, i=i, yt=yt, l=l: e.scalar_tensor_tensor(
                out=yt[:, 0:T], in0=cc[:, i * T:(i + 1) * T], scalar=V(("cg", l), i), in1=rstd[:], op0=ALU.mult, op1=ALU.mult),
                reads=[("cc", i), ("ccs", i), "rstd", "vecs"], writes=[ytr])
            S.op("act", lambda e, i=i, yt=yt: e.activation(out=mixbf(2 * i), in_=yt[:, 0:T], func=AF.Silu),
                 reads=[ytr], writes=[("cc", i)])
        out_proj_partial(l, lambda k: mixbf(2 * k), lambda k: [("cc", k)], 10, 6)

        FP_OFF, FS_OFF = 2, 2 + NP
        zero_halos()
        GSZ = [6, 6, 6, 6, 5, 5, 5, 5]
        GOFF = [sum(GSZ[:g]) for g in range(len(GSZ))]
        ngrp = len(GSZ)
        ectr = {"i": 0}

        def ffn_state_load(grp):
            f0 = GOFF[grp]
            nf = GSZ[grp]
            st, sr = stT[grp % 2], ("Es", grp % 2)
            for side in range(2):
                c0 = side * D_FF + f0 * 128
                src = stf_d[l, :, c0:c0 + nf * 128].rearrange("r (ch q c) -> q r ch c", q=4, c=32)
                for q in range(4):
                    S.op("sp", lambda e, st=st, src=src, q=q, side=side, nf=nf: e.dma_start(
                        out=st[32 * q:32 * q + 32, side * nf:(side + 1) * nf, :], in_=src[q]),
                        writes=[sr], dma=True)

        ffn_late = []

        def ffn_up(grp):
            run_ffn_late()
            f0 = GOFF[grp]
            nf = GSZ[grp]
            ab = (grp % 2) * GF
            st, sr = stT[grp % 2], ("Es", grp % 2)
            if grp + 1 < ngrp:
                ffn_state_load(grp + 1)
            for fi in range(nf):
                f = f0 + fi
                for side in range(2):
                    col0 = side * D_FF + f * 128
                    uc = side * 44 + f
                    ch = side * nf + fi
                    Ei, Er = Ebuf[ectr["i"] % 3], ("E", ectr["i"] % 3)
                    ectr["i"] += 1
                    S.op("dve", lambda e, Ei=Ei, st=st, ch=ch: e.transpose(out=Ei[:, FS_OFF:FS_OFF + 32], in_=st[:, ch, :]),
                         reads=[sr], writes=[Er])
                    psn = proj_chunk(w_up_d, l, col0, lambda k: hT[:, k, :], H_RES)
                    ps = PS[psn]
                    S.op("act", lambda e, ps=ps, Ei=Ei: e.activation(out=Ei[:, FP_OFF:FP_OFF + NP], in_=ps[:, 0:NP], func=AF.Copy),
                         reads=[psn], writes=[Er])
                    S.op("act", lambda e, ps=ps, Ei=Ei: e.activation(out=Ei[:, FS_OFF + 32:FS_OFF + 160], in_=ps[:, NP:T], func=AF.Copy),
                         reads=[psn], writes=[Er])
                    if side == 1:
                        S.op("act", lambda e: e.activation(out=tbuf[0][:, 0:T], in_=tbuf[0][:, 0:T], func=AF.Silu),
                             reads=[("t", 0)], writes=[("t", 0)])
                    def dve_tail(Ei=Ei, Er=Er, ch=ch, side=side, uc=uc):
                        S.op("dve", lambda e: e.transpose(out=oT3[:, ch, :], in_=Ei[:, FS_OFF + 32 + 96:FS_OFF + 160]),
                             reads=[Er], writes=[("Es", 2)])
                        S.op("dve", lambda e: e.transpose(out=oP3[:, ch, :], in_=Ei[:, FP_OFF + NP - 32:FP_OFF + NP]),
                             reads=[Er], writes=STG_ALL)
                        conv_taps(tbuf[side], Ei, 0, FS_OFF, "fw", l, uc, 3, [Er], [("t", side)])

                    def mult(m=ab + fi):
                        S.op("dve", lambda e: e.tensor_tensor(out=mixbf(m), in0=tbuf[0][:, 0:T], in1=tbuf[1][:, 0:T], op=ALU.mult),
                             reads=[("t", 0), ("t", 1)], writes=[ccres_bf(m)])

                    last = (fi == nf - 1 and side == 1)
                    if not last:
                        dve_tail()
                        if side == 1:
                            mult()

            def stores(f0=f0, nf=nf):
                for side in range(2):
                    c0 = side * D_FF + f0 * 128
                    dss = o_ffn_s[l, :, c0:c0 + nf * 128].rearrange("r (ch q c) -> q r ch c", q=4, c=32)
                    dsp = o_ffn_p[l, :, c0:c0 + nf * 128].rearrange("r (ch q c) -> q r ch c", q=4, c=32)
                    for q in range(4):
                        S.op("sp", lambda e, dss=dss, q=q, side=side: e.dma_start(
                            out=dss[q], in_=oT3[32 * q:32 * q + 32, side * nf:(side + 1) * nf, :]),
                            reads=[("Es", 2)], dma=True)
                        S.op("sp", lambda e, dsp=dsp, q=q, side=side: e.dma_start(
                            out=dsp[q], in_=oP3[32 * q + 30:32 * q + 32, side * nf:(side + 1) * nf, :]),
                            reads=STG_ALL, dma=True)

            def late(dve_tail=dve_tail, mult=mult, stores=stores):
                dve_tail()
                mult()
                stores()
            ffn_late.append(late)

        def run_ffn_late():
            pend = ffn_late[:]
            ffn_late.clear()
            for f in pend:
                f()

        def ffn_down(grp):
            f0 = GOFF[grp]
            nf = GSZ[grp]
            ab = (grp % 2) * GF
            act_res = sorted(set(ccres_bf(ab + k) for k in range(nf)))
            for dc in range(16):
                base, wres = wload(w_down_d[l, f0 * 128:(f0 + nf) * 128, dc * 128:(dc + 1) * 128].rearrange("(kc p) c -> p kc c", p=128), nf)
                psn = next_ps()
                pend = deferred[:]
                deferred.clear()
                mm_group(psn, lambda k, base=base: wr[:, base + k, :], lambda k, ab=ab: mixbf(ab + k), nf, reads=wres + act_res)
                for fdef in pend:
                    fdef()
                ps = PS[psn]
                S.op("dve", lambda e, ps=ps, dc=dc: e.tensor_tensor(out=xT[:, dc, :], in0=ps[:, 0:T], in1=xT[:, dc, :], op=ALU.add),
                     reads=[psn, ("x", dc)], writes=[("x", dc)])
                if dc == 1:
                    run_ffn_late()

        ffn_state_load(0)
        rmsnorm_to_h(("n2g", l))
        ffn_up(0)
        for grp in range(ngrp):
            if grp + 1 < ngrp:
                ffn_up(grp + 1)
            ffn_down(grp)
        run_ffn_late()
        flush_deferred()

        S.op("sp", lambda e, l=l: e.dma_start(out=o_pool_s[l, 0:7 * 16, :], in_=stp_d[l, 8 * 16:15 * 16, :]), dma=True)
        S.op("sp", lambda e, l=l: e.dma_start(out=o_cconv_s[l, 0:22 * 16, :], in_=stc_d[l, 8 * 16:30 * 16, :]), dma=True)

    norm_stats(lambda c: xT[:, c, :], lambda c: [("x", c)], 16, 1.0 / D)
    for c in range(16):
        S.op("dve", lambda e, c=c: e.scalar_tensor_tensor(
            out=xT[:, c, :], in0=xT[:, c, :], scalar=V(("fng",), c), in1=rstd[:], op0=ALU.mult, op1=ALU.mult),
            reads=[("x", c), "rstd", "vecs"], writes=[("x", c)])
    XRES = [("x", c) for c in range(16)]
    for ti, (src, r0, n, col0) in enumerate(tiles):
        dst = yp_d if src is xp_d else ys_d
        xi = xin[ti % 3]
        xr = xin_res[ti % 3]
        for half in range(2):
            psn = next_ps()
            ps = PS[psn]
            for j in range(8):
                kc = half * 8 + j
                S.op("pe", lambda e, ps=ps, j=j, kc=kc, n=n, col0=col0: e.transpose(
                    out=ps[0:n, j * 128:(j + 1) * 128], in_=xT[:, kc, col0:col0 + n], identity=identF[:]),
                    reads=[("x", kc), "identF"], writes=[psn])
            if half == 0:
                S.op("act", lambda e, ps=ps, xi=xi, n=n: e.activation(out=xi[0:n, 0:1024], in_=ps[0:n, 0:1024], func=AF.Copy),
                     reads=[psn], writes=xr)
            else:
                S.op("dve", lambda e, ps=ps, xi=xi, n=n: e.tensor_copy(out=xi[0:n, 1024:2048], in_=ps[0:n, 0:1024]),
                     reads=[psn], writes=xr)
        S.op("sp", lambda e, xi=xi, dst=dst, r0=r0, n=n: e.dma_start(out=dst[r0:r0 + n, :], in_=xi[0:n, :]),
             reads=xr, dma=True)

    fns = {"pe": block.tensor, "act": block.scalar, "dve": block.vector, "pool": block.gpsimd, "sp": block.sync}
    S.emit(fns, sems, dsems)
    es.close()
    return nc


def _jmajor(a):
    return np.ascontiguousarray(np.transpose(a, (1, 0, 2))).reshape(a.shape[1] * a.shape[0], a.shape[2])


def _unjmajor(a, R):
    return np.ascontiguousarray(np.transpose(a.reshape(R, 16, a.shape[-1]), (1, 0, 2)))


def kernel(**inp):
    inp = {k: np.asarray(v) for k, v in inp.items()}
    if "nc" not in _NC_CACHE:
        _NC_CACHE["nc"] = build_program()
    nc = _NC_CACHE["nc"]
    vecs = _pack_vecs(inp)
    ident = np.eye(128, dtype=np.float32)
    in_maps = []
    for c in range(NCORES):
        b, half = c // 2, c % 2
        t0 = 0 if half == 0 else SEQ - NP
        cnt = np.zeros((128, 4, 16), np.float32)
        for g, k in enumerate((2, 4, 8, 16)):
            for t in range(16):
                cnt[:, g, t] = 1.0 / min(t0 + t + 1, k)
        sl = slice(16 * c, 16 * c + 16)
        m = {
            "xp": np.ascontiguousarray(inp["x_prompt"][b, t0:t0 + NP]),
            "xs": _jmajor(inp["x_sample"][sl]),
            "st_pool": np.stack([_jmajor(inp["state_pool"][l, sl]) for l in range(DEPTH)]),
            "st_sconv": np.stack([_jmajor(inp["state_sconv"][l, sl]) for l in range(DEPTH)]),
            "st_cconv": np.stack([_jmajor(inp["state_cconv"][l, sl]) for l in range(DEPTH)]),
            "st_ffn": np.stack([_jmajor(inp["state_ffn"][l, sl]) for l in range(DEPTH)]),
            "vecs": vecs,
            "cnt": cnt.reshape(128, 64),
            "ident": ident,
            "w_in": inp["w_in"],
            "pool_w": inp["pool_w"],
            "w_out": inp["w_out"],
            "w_up": inp["w_up"],
            "w_down": inp["w_down"],
        }
        in_maps.append(m)
    res = run_bass_kernel_spmd(nc, in_maps, core_ids=list(range(NCORES)))
    R = res.results
    B = 4
    y_prompt = np.zeros((B, SEQ, D), np.float32)
    y_sample = np.zeros((128, 8, D), np.float32)
    new_pool_p = np.zeros((DEPTH, B, 15, D_POOL), np.float32)
    new_sconv_p = np.zeros((DEPTH, B, 2, D_SC), np.float32)
    new_cconv_p = np.zeros((DEPTH, B, 30, D_CC), np.float32)
    new_ffn_p = np.zeros((DEPTH, B, 2, 2 * D_FF), np.float32)
    new_pool_s = np.zeros((DEPTH, 128, 15, D_POOL), np.float32)
    new_sconv_s = np.zeros((DEPTH, 128, 2, D_SC), np.float32)
    new_cconv_s = np.zeros((DEPTH, 128, 30, D_CC), np.float32)
    new_ffn_s = np.zeros((DEPTH, 128, 2, 2 * D_FF), np.float32)
    for c in range(NCORES):
        b, half = c // 2, c % 2
        r = R[c]
        sl = slice(16 * c, 16 * c + 16)
        if half == 0:
            y_prompt[b, 0:NP] = r["yp"]
        else:
            y_prompt[b, NP:SEQ] = r["yp"][2 * NP - SEQ:]
            for l in range(DEPTH):
                new_pool_p[l, b] = r["o_pool_p"][l]
                new_sconv_p[l, b] = r["o_sconv_p"][l]
                new_cconv_p[l, b] = r["o_cconv_p"][l]
                new_ffn_p[l, b] = r["o_ffn_p"][l]
        y_sample[sl] = _unjmajor(r["ys"], 8)
        for l in range(DEPTH):
            new_pool_s[l, sl] = _unjmajor(r["o_pool_s"][l], 15)
            new_sconv_s[l, sl] = _unjmajor(r["o_sconv_s"][l], 2)
            new_cconv_s[l, sl] = _unjmajor(r["o_cconv_s"][l], 30)
            new_ffn_s[l, sl] = _unjmajor(r["o_ffn_s"][l], 2)
    return (y_prompt, y_sample, new_pool_p, new_sconv_p, new_cconv_p, new_ffn_p,
            new_pool_s, new_sconv_s, new_cconv_s, new_ffn_s)
```
